# Optimizing a Trainium2 kernel written in Bass

```python
import math
import jax, jax.numpy as jnp
from jax import lax
import numpy as np

D_MODEL = 1024
BATCH = 4
SEQ = 8192
DEPTH = 2

GRID_W = 64
Q_BLOCK = 128
RET_CHUNK = 128
NORM_EPS = 1e-6

DIFF_HEADS = 4
DIFF_HEAD_DIM = 64
DIFF_V_DIM = 2 * DIFF_HEAD_DIM
DIFF_WIDTH = DIFF_HEADS * DIFF_V_DIM
PARTIAL_ROT_DIM = DIFF_HEAD_DIM // 4
PARTIAL_ROPE_THETA = 500000.0

RET_HEADS = 4
RET_KEY_DIM = 128
RET_VAL_DIM = 128
RET_WIDTH = RET_HEADS * RET_VAL_DIM
RET_ROPE_THETA = 10000.0

GQA_Q_HEADS = 8
GQA_KV_HEADS = 2
GQA_GROUP = GQA_Q_HEADS // GQA_KV_HEADS
GQA_HEAD_DIM = 64
GQA_WIDTH = GQA_Q_HEADS * GQA_HEAD_DIM
AXIAL_DIM = GQA_HEAD_DIM // 2
AXIAL_ROPE_THETA = 10000.0

N_BRANCHES = 3
BRANCH_WIDTH = 512
D_FF = 4 * D_MODEL

IN_SPLITS = (
    DIFF_HEADS * 2 * DIFF_HEAD_DIM,
    DIFF_HEADS * 2 * DIFF_HEAD_DIM,
    DIFF_WIDTH,
    RET_HEADS * RET_KEY_DIM,
    RET_HEADS * RET_KEY_DIM,
    RET_WIDTH,
    RET_WIDTH,
    GQA_Q_HEADS * GQA_HEAD_DIM,
    GQA_KV_HEADS * GQA_HEAD_DIM,
    GQA_KV_HEADS * GQA_HEAD_DIM,
    N_BRANCHES * D_MODEL,
)
IN_COLS = 7424

kernel_name = "gated_parallel_diffattn_retention_axialgqa_encoder"


def rms_norm(x, gain, eps=NORM_EPS):
    xf = x.astype(jnp.float32)
    y = xf * lax.rsqrt(jnp.mean(xf * xf, axis=-1, keepdims=True) + eps)
    return (y * gain.astype(jnp.float32)).astype(x.dtype)


def rope_cos_sin(pos, dim, theta):
    inv_freq = theta ** (-jnp.arange(0, dim, 2, dtype=jnp.float32) / dim)
    ang = pos[:, None] * inv_freq[None, :]
    return jnp.cos(ang), jnp.sin(ang)


def apply_rope(x, cos_sin):
    cos, sin = cos_sin
    n = x.shape[-1] // 2
    shape = (1, cos.shape[0]) + (1,) * (x.ndim - 3) + (n,)
    c = cos.reshape(shape).astype(x.dtype)
    s = sin.reshape(shape).astype(x.dtype)
    x1, x2 = x[..., :n], x[..., n:]
    return jnp.concatenate([x1 * c - x2 * s, x2 * c + x1 * s], axis=-1)


def split_cols(y, sizes):
    out, start = [], 0
    for n in sizes:
        out.append(y[..., start:start + n])
        start += n
    return out


def sweep_query_blocks(fn, q):
    B, S = q.shape[:2]
    nb = S // Q_BLOCK
    qb = jnp.moveaxis(q.reshape((B, nb, Q_BLOCK) + q.shape[2:]), 1, 0)
    ob = lax.map(fn, qb)
    return jnp.moveaxis(ob, 0, 1).reshape((B, S) + ob.shape[3:])


def differential_attention(q, k, v, lam):
    scale = DIFF_HEAD_DIM ** -0.5

    def block(qb):
        s = jnp.einsum('bqhmd,bkhmd->bhmqk', qb, k).astype(jnp.float32) * scale
        p = jax.nn.softmax(s, axis=-1)
        w = (p[:, :, 0] - lam * p[:, :, 1]).astype(v.dtype)
        return jnp.einsum('bhqk,bkhe->bqhe', w, v)

    return sweep_query_blocks(block, q)


def grouped_query_attention(q, k, v):
    scale = GQA_HEAD_DIM ** -0.5

    def block(qb):
        s = jnp.einsum('bqgrd,bkgd->bgrqk', qb, k).astype(jnp.float32) * scale
        p = jax.nn.softmax(s, axis=-1).astype(v.dtype)
        return jnp.einsum('bgrqk,bkgd->bqgrd', p, v)

    return sweep_query_blocks(block, q)


def bidirectional_retention(q, k, v, log_fwd, log_bwd):
    B, S, H, dk = q.shape
    dv = v.shape[-1]
    C = RET_CHUNK
    N = S // C
    q = q.reshape(B, N, C, H, dk)
    k = k.reshape(B, N, C, H, dk)
    v = v.reshape(B, N, C, H, dv)
    i = jnp.arange(C, dtype=jnp.float32)
    diff = i[:, None] - i[None, :]
    decay_intra = jnp.where(
        diff[None] >= 0,
        jnp.exp(jnp.maximum(diff, 0.0)[None] * log_fwd[:, None, None]),
        jnp.exp(jnp.maximum(-diff, 0.0)[None] * log_bwd[:, None, None]))
    scores = jnp.einsum('bnihd,bnjhd->bnhij', q, k) * decay_intra
    out = jnp.einsum('bnhij,bnjhe->bnihe', scores, v)

    zeta_f = jnp.exp((C - 1 - i)[None] * log_fwd[:, None])
    xi_f = jnp.exp((i + 1)[None] * log_fwd[:, None])
    zeta_b = jnp.exp(i[None] * log_bwd[:, None])
    xi_b = jnp.exp((C - i)[None] * log_bwd[:, None])
    kv_f = jnp.einsum('bnjhd,bnjhe,hj->nbhde', k, v, zeta_f)
    kv_b = jnp.einsum('bnjhd,bnjhe,hj->nbhde', k, v, zeta_b)
    dec_f = jnp.exp(C * log_fwd)[None, :, None, None]
    dec_b = jnp.exp(C * log_bwd)[None, :, None, None]

    def chunk_states(kv, decay, reverse):
        def step(state, kv_n):
            return state * decay + kv_n, state
        _, states = lax.scan(step, jnp.zeros_like(kv[0]), kv, reverse=reverse)
        return states

    s_f = chunk_states(kv_f, dec_f, False)
    s_b = chunk_states(kv_b, dec_b, True)
    out = (out
           + jnp.einsum('bnihd,nbhde,hi->bnihe', q, s_f, xi_f)
           + jnp.einsum('bnihd,nbhde,hi->bnihe', q, s_b, xi_b))
    return out.reshape(B, S, H, dv)


def head_group_norm(x, gain, eps=1e-5):
    xf = x.astype(jnp.float32)
    mu = jnp.mean(xf, axis=-1, keepdims=True)
    var = jnp.mean(jnp.square(xf - mu), axis=-1, keepdims=True)
    y = ((xf - mu) * lax.rsqrt(var + eps)).reshape(x.shape[0], x.shape[1], -1)
    return y * gain.astype(jnp.float32)


def setup_inputs(seed: int = 0) -> dict:
    key = jax.random.key(seed)
    ks = jax.random.split(key, 21)
    L, D = DEPTH, D_MODEL
    f32 = jnp.float32

    def normal(k, shape, scale):
        return jax.random.normal(k, shape, f32) * scale

    def gain(k, shape):
        return 1.0 + 0.02 * jax.random.normal(k, shape, f32)

    base_log_decay = jnp.log(-jnp.log1p(-(2.0 ** (-5.0 - jnp.arange(RET_HEADS, dtype=f32)))))
    return {
        "x": jax.random.normal(ks[0], (BATCH, SEQ, D), f32),
        "attn_norm": gain(ks[1], (L, D)),
        "w_in": normal(ks[2], (L, D, IN_COLS), D ** -0.5),
        "diff_q_norm": gain(ks[3], (L, DIFF_HEAD_DIM)),
        "diff_k_norm": gain(ks[4], (L, DIFF_HEAD_DIM)),
        "diff_lam_q1": normal(ks[5], (L, DIFF_HEAD_DIM), 0.1),
        "diff_lam_k1": normal(ks[6], (L, DIFF_HEAD_DIM), 0.1),
        "diff_lam_q2": normal(ks[7], (L, DIFF_HEAD_DIM), 0.1),
        "diff_lam_k2": normal(ks[8], (L, DIFF_HEAD_DIM), 0.1),
        "diff_subln": gain(ks[9], (L, DIFF_V_DIM)),
        "ret_decay_fwd": base_log_decay[None] + normal(ks[10], (L, RET_HEADS), 0.1),
        "ret_decay_bwd": base_log_decay[None] + normal(ks[11], (L, RET_HEADS), 0.1),
        "ret_group_norm": gain(ks[12], (L, RET_WIDTH)),
        "gqa_q_norm": gain(ks[13], (L, GQA_HEAD_DIM)),
        "gqa_k_norm": gain(ks[14], (L, GQA_HEAD_DIM)),
        "w_branch": normal(ks[15], (L, N_BRANCHES, BRANCH_WIDTH, D), BRANCH_WIDTH ** -0.5),
        "w_out": normal(ks[16], (L, D, D), D ** -0.5),
        "mlp_norm": gain(ks[17], (L, D)),
        "w_mlp_in": normal(ks[18], (L, D, D_FF), D ** -0.5),
        "w_mlp_out": normal(ks[19], (L, D_FF, D), D_FF ** -0.5),
    }


def reference(x, attn_norm, w_in, diff_q_norm, diff_k_norm, diff_lam_q1, diff_lam_k1,
              diff_lam_q2, diff_lam_k2, diff_subln, ret_decay_fwd, ret_decay_bwd,
              ret_group_norm, gqa_q_norm, gqa_k_norm, w_branch, w_out, mlp_norm,
              w_mlp_in, w_mlp_out):
    B, S, D = x.shape
    f32 = jnp.float32
    rows = S // GRID_W
    seq_pos = jnp.arange(S, dtype=f32)
    row_pos = jnp.repeat(jnp.arange(rows, dtype=f32), GRID_W)
    col_pos = jnp.tile(jnp.arange(GRID_W, dtype=f32), rows)
    rope_partial = rope_cos_sin(seq_pos, PARTIAL_ROT_DIM, PARTIAL_ROPE_THETA)
    rope_ret = rope_cos_sin(seq_pos, RET_KEY_DIM, RET_ROPE_THETA)
    rope_row = rope_cos_sin(row_pos, AXIAL_DIM, AXIAL_ROPE_THETA)
    rope_col = rope_cos_sin(col_pos, AXIAL_DIM, AXIAL_ROPE_THETA)

    def partial_rope(t):
        return jnp.concatenate([apply_rope(t[..., :PARTIAL_ROT_DIM], rope_partial),
                                t[..., PARTIAL_ROT_DIM:]], axis=-1)

    def axial_rope(t):
        return jnp.concatenate([apply_rope(t[..., :AXIAL_DIM], rope_row),
                                apply_rope(t[..., AXIAL_DIM:], rope_col)], axis=-1)

    h = x
    for l in range(DEPTH):
        u = rms_norm(h, attn_norm[l])
        proj = jnp.einsum('bsd,dc->bsc', u, w_in[l])
        (dq, dk, dv, rq, rk, rv, rg, gq, gk, gv, gate_logits) = split_cols(proj, IN_SPLITS)

        lambda_init = 0.8 - 0.6 * math.exp(-0.3 * l)
        lam = (jnp.exp(jnp.sum(diff_lam_q1[l].astype(f32) * diff_lam_k1[l].astype(f32)))
               - jnp.exp(jnp.sum(diff_lam_q2[l].astype(f32) * diff_lam_k2[l].astype(f32)))
               + lambda_init)
        dq = partial_rope(rms_norm(dq.reshape(B, S, DIFF_HEADS, 2, DIFF_HEAD_DIM), diff_q_norm[l]))
        dk = partial_rope(rms_norm(dk.reshape(B, S, DIFF_HEADS, 2, DIFF_HEAD_DIM), diff_k_norm[l]))
        dv = dv.reshape(B, S, DIFF_HEADS, DIFF_V_DIM)
        a = differential_attention(dq, dk, dv, lam)
        a = (rms_norm(a, diff_subln[l], eps=1e-5) * (1.0 - lambda_init)).reshape(B, S, DIFF_WIDTH)

        rq = apply_rope(rq.reshape(B, S, RET_HEADS, RET_KEY_DIM), rope_ret).astype(f32)
        rk = (apply_rope(rk.reshape(B, S, RET_HEADS, RET_KEY_DIM), rope_ret).astype(f32)
              * (RET_KEY_DIM ** -0.5))
        rv = rv.reshape(B, S, RET_HEADS, RET_VAL_DIM).astype(f32)
        log_fwd = -jnp.exp(ret_decay_fwd[l].astype(f32))
        log_bwd = -jnp.exp(ret_decay_bwd[l].astype(f32))
        r = bidirectional_retention(rq, rk, rv, log_fwd, log_bwd)
        r = (jax.nn.silu(rg.astype(f32)) * head_group_norm(r, ret_group_norm[l])).astype(h.dtype)

        gq = axial_rope(rms_norm(gq.reshape(B, S, GQA_KV_HEADS, GQA_GROUP, GQA_HEAD_DIM), gqa_q_norm[l]))
        gk = axial_rope(rms_norm(gk.reshape(B, S, GQA_KV_HEADS, GQA_HEAD_DIM), gqa_k_norm[l]))
        gv = gv.reshape(B, S, GQA_KV_HEADS, GQA_HEAD_DIM)
        c = grouped_query_attention(gq, gk, gv).reshape(B, S, GQA_WIDTH)

        branches = jnp.stack([a, r, c], axis=2)
        projected = jnp.einsum('bsnw,nwd->bsnd', branches, w_branch[l])
        gates = jax.nn.sigmoid(gate_logits.reshape(B, S, N_BRANCHES, D))
        merged = jnp.sum(gates * projected, axis=2)
        h = h + jnp.einsum('bsd,de->bse', merged, w_out[l])

        m = jnp.einsum('bsd,df->bsf', rms_norm(h, mlp_norm[l]), w_mlp_in[l])
        m = jnp.square(jax.nn.relu(m))
        h = h + jnp.einsum('bsf,fd->bsd', m, w_mlp_out[l])
    return h
```

```python
import math
import contextlib
import numpy as np
import concourse.bass as bass
import concourse.mybir as mybir
from concourse.bass_utils import run_bass_kernel_spmd

F32 = mybir.dt.float32
BF16 = mybir.dt.bfloat16
AF = mybir.ActivationFunctionType
ALU = mybir.AluOpType
AX = mybir.AxisListType

NCORES = 8
T = 4096
SEQ = 8192
D = 1024
NT = T // 128
INC = 7424
DFF = 4096
EPS = 1e-6
PAIRS = [[0, 1], [2, 3], [4, 5], [6, 7]]


class Op:
    __slots__ = ("eng", "fn", "deps", "dma", "milestone", "count", "sem", "cc")

    def __init__(self, eng, fn, dma):
        self.eng = eng
        self.fn = fn
        self.dma = dma
        self.cc = False
        self.deps = []
        self.milestone = False
        self.count = 0
        self.sem = None


class Sched:
    ENGS = ("pe", "act", "dve", "pool", "sp")

    def __init__(self, nc, n_dma_sems=16):
        self.nc = nc
        self.ops = {e: [] for e in self.ENGS}
        self.lastw = {}
        self.readers = {}
        self.n_dma_sems = n_dma_sems
        self.dmas = []

    def op(self, eng, fn, reads=(), writes=(), dma=False):
        o = Op(eng, fn, dma)
        deps = {}
        for t in reads:
            w = self.lastw.get(t)
            if w is not None:
                deps[id(w)] = (w, True)
        for t in writes:
            w = self.lastw.get(t)
            if w is not None and id(w) not in deps:
                deps[id(w)] = (w, w.dma or w.cc)
            for r in self.readers.get(t, ()):
                if id(r) not in deps:
                    deps[id(r)] = (r, r.dma or r.cc)
        for (d, raw) in deps.values():
            if d.eng == eng and not (d.dma or d.cc) and not raw:
                continue
            o.deps.append(d)
            if not d.dma:
                d.milestone = True
        for t in reads:
            self.readers.setdefault(t, []).append(o)
        for t in writes:
            self.lastw[t] = o
            self.readers[t] = []
        self.ops[eng].append(o)
        if dma:
            self.dmas.append(o)
        return o

    def barrier(self):
        lasts = []
        for e in self.ENGS:
            for o in reversed(self.ops[e]):
                if not o.dma:
                    lasts.append(o)
                    break
        for e in self.ENGS:
            j = Op(e, lambda eng: eng.nop(), False)
            for o in lasts:
                if o.eng != e:
                    j.deps.append(o)
                    o.milestone = True
            j.deps.extend(self.dmas)
            self.ops[e].append(j)
        self.dmas = []
        self.lastw = {}
        self.readers = {}

    def emit(self, stack):
        nc = self.nc
        esem = {e: stack.enter_context(nc.semaphore("s_" + e)) for e in self.ENGS}
        ccsem = stack.enter_context(nc.semaphore("s_cc"))
        ccn = 0
        dsem = {}
        for e in self.ENGS:
            if any(o.dma for o in self.ops[e]):
                dsem[e] = [stack.enter_context(nc.semaphore(f"d_{e}{i}")) for i in range(self.n_dma_sems)]
        for e in self.ENGS:
            c = 0
            k = 0
            uses = [0] * self.n_dma_sems
            for o in self.ops[e]:
                if o.cc:
                    ccn += 1
                    o.sem = ccsem
                    o.count = ccn
                    o.milestone = True
                elif o.dma:
                    j = k % self.n_dma_sems
                    uses[j] += 1
                    o.sem = dsem[e][j]
                    o.count = 16 * uses[j]
                    k += 1
                elif o.milestone:
                    c += 1
                    o.sem = esem[e]
                    o.count = c
        block = stack.enter_context(nc.Block())

        def run(ename, eng):
            waited = {}
            for o in self.ops[ename]:
                need = {}
                for d in o.deps:
                    key = id(d.sem)
                    if waited.get(key, 0) >= d.count:
                        continue
                    if key not in need or need[key][1] < d.count:
                        need[key] = (d.sem, d.count)
                if o.dma:
                    key = id(o.sem)
                    pv = o.count - 16
                    if pv > 0 and waited.get(key, 0) < pv:
                        if key not in need or need[key][1] < pv:
                            need[key] = (o.sem, pv)
                for key, (s, v) in need.items():
                    eng.wait_ge(s, v)
                    waited[key] = v
                ins = o.fn(eng)
                if o.dma:
                    ins.then_inc(o.sem, 16)
                elif o.milestone:
                    ins.then_inc(o.sem, 1)

        @block.tensor
        def _(e):
            run("pe", e)

        @block.scalar
        def _(e):
            run("act", e)

        @block.vector
        def _(e):
            run("dve", e)

        @block.gpsimd
        def _(e):
            run("pool", e)

        @block.sync
        def _(e):
            run("sp", e)


def build(n_layers=2, dbg=False, stop=99):
    nc = bass.Bass("TRN2", target_bir_lowering=False)
    S = Sched(nc)

    def din(name, shape, dt=F32):
        return nc.dram_tensor(name, list(shape), dt, kind="ExternalInput").ap()

    x_in = din("x", [T, D])
    p_attn_norm = din("attn_norm", [2, D])
    w_in = din("w_in", [2, D, INC])
    p_dqn = din("diff_q_norm", [2, 64])
    p_dkn = din("diff_k_norm", [2, 64])
    p_lq1 = din("diff_lam_q1", [2, 64])
    p_lk1 = din("diff_lam_k1", [2, 64])
    p_lq2 = din("diff_lam_q2", [2, 64])
    p_lk2 = din("diff_lam_k2", [2, 64])
    p_subln = din("diff_subln", [2, 128])
    p_rdf = din("ret_decay_fwd", [2, 4])
    p_rdb = din("ret_decay_bwd", [2, 4])
    p_rgn = din("ret_group_norm", [2, 512])
    p_gqn = din("gqa_q_norm", [2, 64])
    p_gkn = din("gqa_k_norm", [2, 64])
    w_branch = din("w_branch", [2, 3, 512, D])
    w_out = din("w_out", [2, D, D])
    p_mlp_norm = din("mlp_norm", [2, D])
    w_mlp_in = din("w_mlp_in", [2, D, DFF])
    w_mlp_out = din("w_mlp_out", [2, DFF, D])
    c_ident = din("c_ident", [128, 128])
    c_rp = din("c_rp", [T, 16])
    c_rr = din("c_rr", [T, 256])
    c_ax = din("c_ax", [T, 64])
    c_pat = din("c_pat", [128, 4 * 128 + 2 + 64 + 64])
    c_flags = din("c_flags", [128, 2])

    out_ext = nc.dram_tensor("out", [T, D], F32, kind="ExternalOutput").ap()

    def dscr(name, shape, dt=BF16):
        return nc.dram_tensor(name, list(shape), dt)

    QA = dscr("QA", [512, T]).ap()
    KAi = [dscr(f"KA{i}", [256, T]) for i in range(2)]; KAoi = [dscr(f"KAo{i}", [512, T]) for i in range(2)]
    KC = dscr("KC", [256, T]); KCo = dscr("KCo", [512, T])
    VAi = [dscr(f"VA{i}", [T // 2, 512]) for i in range(2)]; VAoi = [dscr(f"VAo{i}", [T, 512]) for i in range(2)]
    VC = dscr("VC", [T, 128]); VCo = dscr("VCo", [2 * T, 128])
    ST = dscr("ST", [1024, 128], F32); STo = dscr("STo", [2048, 128], F32)
    QC = dscr("QC", [512, T]).ap()
    RQT = dscr("RQT", [512, T]).ap()
    RKT = dscr("RKT", [512, T]).ap()
    RKtok = dscr("RKtok", [T, 512]).ap()
    RVtok = dscr("RVtok", [T, 512]).ap()
    RG = dscr("RG", [T, 512]).ap()
    GATES = dscr("GATES", [T, 3072]).ap()
    BR = dscr("BR", [3, 512, T]).ap()
    H1 = dscr("H1", [T, D], F32).ap()
    Hs = dscr("Hs", [T, D], F32).ap()
    U2T = dscr("U2T", [D, T]).ap()
    if dbg:
        dbg_out = {}
        for nm, shp, dt in (("d_BR", [3, 512, T], BF16), ("d_H1", [T, D], F32), ("d_STo", [2048, 128], F32)):
            dbg_out[nm] = nc.dram_tensor(nm, shp, dt, kind="ExternalOutput").ap()

    def phase_ctx(i):
        if stop >= i:
            with contextlib.ExitStack() as ph_:
                yield ph_

    uid = [0]

    def U(p):
        uid[0] += 1
        return f"{p}{uid[0]}"

    def DMA(out, in_, reads=(), writes=(), q="sp"):
        return S.op(q, lambda e: e.dma_start(out=out, in_=in_), reads, writes, dma=True)

    def DMA3(out, in_, reads=(), writes=(), step=8):
        n = out.shape[1]
        for a in range(0, n, step):
            DMA(out[:, a:a + step, :], in_[:, a:a + step, :], reads, [w + f"_{a}" for w in writes])
        return [w + f"_{a}" for w in writes for a in range(0, n, step)]

    def MM(out, lhsT, rhs, start, stop, reads, writes):
        return S.op("pe", lambda e: e.matmul(out, lhsT=lhsT, rhs=rhs, start=start, stop=stop), reads, writes)

    def TR(out, in_, ident, reads, writes):
        return S.op("pe", lambda e: e.transpose(out, in_, ident), reads, writes)

    def ACT(out, in_, func, reads, writes, scale=1.0, bias=0.0, accum=None):
        if accum is None:
            return S.op("act", lambda e: e.activation(out=out, in_=in_, func=func, bias=bias, scale=scale), reads, writes)
        return S.op("act", lambda e: e.activation(out=out, in_=in_, func=func, bias=bias, scale=scale, accum_out=accum), reads, writes)

    def ACOPY(out, in_, reads, writes):
        return S.op("act", lambda e: e.copy(out=out, in_=in_), reads, writes)

    def TT(eng, out, in0, in1, op, reads, writes):
        return S.op(eng, lambda e: e.tensor_tensor(out=out, in0=in0, in1=in1, op=op), reads, writes)

    def TS(eng, out, in0, s1, s2, op0, op1, reads, writes):
        assert eng == "dve"
        if s2 is None:
            return S.op(eng, lambda e: e.tensor_scalar(out=out, in0=in0, scalar1=s1, scalar2=None, op0=op0), reads, writes)
        return S.op(eng, lambda e: e.tensor_scalar(out=out, in0=in0, scalar1=s1, scalar2=s2, op0=op0, op1=op1), reads, writes)

    def STT(eng, out, in0, scalar, in1, op0, op1, reads, writes):
        return S.op(eng, lambda e: e.scalar_tensor_tensor(out=out, in0=in0, scalar=scalar, in1=in1, op0=op0, op1=op1), reads, writes)

    def CP(eng, out, in_, reads, writes):
        return S.op(eng, lambda e: e.tensor_copy(out=out, in_=in_), reads, writes)

    def RSUM(eng, out, in_, reads, writes):
        return S.op(eng, lambda e: e.reduce_sum(out=out, in_=in_, axis=AX.X), reads, writes)

    def MEMSET(eng, ap, val, writes):
        return S.op(eng, lambda e: e.memset(ap, val), (), writes)

    def RECIP(out, in_, reads, writes):
        return S.op("dve", lambda e: e.reciprocal(out=out, in_=in_), reads, writes)

    def CC(in_t, out_t, reads, writes):
        o = S.op("pool", lambda e: e.collective_compute("AllGather", ALU.bypass, replica_groups=PAIRS,
                                                        ins=[in_t.ap().opt()], outs=[out_t.ap().opt()]), reads, writes)
        o.cc = True
        return o

    with contextlib.ExitStack() as top:
        def sbT(st, name, shape, dt=F32):
            return st.enter_context(nc.sbuf_tensor(U(name), list(shape), dt))

        def psT(st, name, shape, dt=F32):
            return st.enter_context(nc.psum_tensor(U(name), list(shape), dt))

        ident_f = sbT(top, "identf", [128, 128])
        ident = sbT(top, "ident", [128, 128], BF16)
        ones_bf = sbT(top, "onesbf", [128, 128], BF16)
        ones_f = sbT(top, "onesf", [128, 128])
        pat = sbT(top, "pat", [128, 4 * 128 + 2 + 64 + 64])
        flags = sbT(top, "flags", [128, 2])
        rpt = sbT(top, "rpt", [128, NT, 16])
        axt = sbT(top, "axt", [128, NT, 64])
        DMA(ident_f[:], c_ident[:, :], (), ["identf"])
        CP("dve", ident[:], ident_f[:], ["identf"], ["ident"])
        MEMSET("dve", ones_bf[:], 1.0, ["onesbf"])
        MEMSET("dve", ones_f[:], 1.0, ["onesf"])
        epsb = sbT(top, "epsb", [128, 1])
        MEMSET("dve", epsb[:], 1e-5, ["epsb"])
        mhalf = sbT(top, "mhalf", [128, 512])
        MEMSET("dve", mhalf[:], -0.5, ["mhalf"])
        DMA(pat[:], c_pat[:, :], (), ["pat"])
        DMA(flags[:], c_flags[:, :], (), ["flags"])
        DMA3(rpt[:], c_rp.rearrange("(n p) c -> p n c", p=128), (), ["rpt"])
        DMA3(axt[:], c_ax.rearrange("(n p) c -> p n c", p=128), (), ["axt"])
        P1 = pat[:, 0:128]
        P2 = pat[:, 128:256]
        IF1 = pat[:, 256:384]
        IF2 = pat[:, 384:512]
        ZF = pat[:, 512:513]
        ZB = pat[:, 513:514]
        T1 = pat[:, 514:546]
        T2 = pat[:, 546:578]
        SEL = pat[:, 578:642]
        S.barrier()

        for l in range(n_layers):
            lam_init = 0.8 - 0.6 * math.exp(-0.3 * l)
            Hsrc = x_in if l == 0 else Hs
            Hdst = out_ext if l == n_layers - 1 else Hs

            with contextlib.ExitStack() as lay:
                g_attn = sbT(lay, "gattn", [128, D])
                g_mlp = sbT(lay, "gmlp", [128, D])
                g_dq = sbT(lay, "gdq", [128, 64])
                g_dk = sbT(lay, "gdk", [128, 64])
                g_gq = sbT(lay, "ggq", [128, 64])
                g_gk = sbT(lay, "ggk", [128, 64])
                lamv = sbT(lay, "lamv", [128, 4, 64])
                lamt = sbT(lay, "lamt", [128, 8])
                nlam = sbT(lay, "nlam", [128, 1])
                subc = sbT(lay, "subc", [128, 2])
                rdec = sbT(lay, "rdec", [128, 8])
                lfb = sbT(lay, "lfb", [128, 8])
                rgn = sbT(lay, "rgn", [128, 512])
                DMA(g_attn[:], p_attn_norm[l:l + 1, :].partition_broadcast(128), (), ["gattn"])
                DMA(g_mlp[:], p_mlp_norm[l:l + 1, :].partition_broadcast(128), (), ["gmlp"])
                DMA(g_dq[:], p_dqn[l:l + 1, :].partition_broadcast(128), (), ["gdq"])
                DMA(g_dk[:], p_dkn[l:l + 1, :].partition_broadcast(128), (), ["gdk"])
                DMA(g_gq[:], p_gqn[l:l + 1, :].partition_broadcast(128), (), ["ggq"])
                DMA(g_gk[:], p_gkn[l:l + 1, :].partition_broadcast(128), (), ["ggk"])
                for i, pv in enumerate((p_lq1, p_lk1, p_lq2, p_lk2)):
                    DMA(lamv[:, i, :], pv[l:l + 1, :].partition_broadcast(128), (), [f"lamv{i}"])
                DMA(subc[:, 0:1], p_subln[l:l + 1, :].rearrange("o p -> p o"), (), ["subc0"])
                DMA(rdec[:, 0:4], p_rdf[l:l + 1, :].partition_broadcast(128), (), ["rdec0"])
                DMA(rdec[:, 4:8], p_rdb[l:l + 1, :].partition_broadcast(128), (), ["rdec1"])
                DMA(rgn[:], p_rgn[l:l + 1, :].partition_broadcast(128), (), ["rgn"])
                TT("dve", lamv[:, 0, :], lamv[:, 0, :], lamv[:, 1, :], ALU.mult, ["lamv0", "lamv1"], ["lamp0"])
                TT("dve", lamv[:, 2, :], lamv[:, 2, :], lamv[:, 3, :], ALU.mult, ["lamv2", "lamv3"], ["lamp1"])
                RSUM("dve", lamt[:, 0:1], lamv[:, 0, :], ["lamp0"], ["lams0"])
                RSUM("dve", lamt[:, 1:2], lamv[:, 2, :], ["lamp1"], ["lams1"])
                ACT(lamt[:, 2:4], lamt[:, 0:2], AF.Exp, ["lams0", "lams1"], ["lame"])
                STT("dve", nlam[:, 0:1], lamt[:, 3:4], -lam_init, lamt[:, 2:3], ALU.add, ALU.subtract, ["lame"], ["nlam"])
                TS("dve", subc[:, 1:2], subc[:, 0:1], 1.0 - lam_init, None, ALU.mult, None, ["subc0"], ["subc1"])
                ACT(lfb[:], rdec[:], AF.Exp, ["rdec0", "rdec1"], ["lfbe"])
                TS("dve", lfb[:], lfb[:], -1.0, None, ALU.mult, None, ["lfbe"], ["lfb"])
                S.barrier()

                for ph in phase_ctx(1):
                    uT = sbT(ph, "uT", [128, 8, T], BF16)
                    ht = [sbT(ph, "ht", [128, D]) for _ in range(2)]
                    junk = sbT(ph, "junk", [128, D], BF16)
                    ss1 = [sbT(ph, "ss1", [128, 2]) for _ in range(2)]
                    ub = [sbT(ph, "ub", [128, D], BF16) for _ in range(2)]
                    ptr = [psT(ph, "ptr", [128, 8, 128], BF16) for _ in range(2)]
                    for ti in range(NT):
                        k = ti % 2
                        DMA(ht[k][:], Hsrc[ti * 128:(ti + 1) * 128, :], (), [f"ht{k}"])
                        ACT(junk[:], ht[k][:], AF.Square, [f"ht{k}"], ["junk", f"ssa{k}"], accum=ss1[k][:, 0:1])
                        TS("dve", ss1[k][:, 1:2], ss1[k][:, 0:1], 1.0 / D, EPS, ALU.mult, ALU.add, [f"ssa{k}"], [f"ssb{k}"])
                        TT("pool", ss1[k][:, 1:2], ss1[k][:, 1:2], mhalf[:, 0:1], ALU.pow, [f"ssb{k}"], [f"ssc{k}"])
                        STT("dve", ub[k][:], ht[k][:], ss1[k][:, 1:2], g_attn[:], ALU.mult, ALU.mult,
                            [f"ht{k}", f"ssc{k}"], [f"ub{k}"])
                        for c in range(8):
                            TR(ptr[k][:, c, :], ub[k][:, c * 128:(c + 1) * 128], ident[:], [f"ub{k}"], [f"ptr{k}"])
                        ACOPY(uT[:, :, ti * 128:(ti + 1) * 128], ptr[k][:], [f"ptr{k}"], ["uT"])

                    wst = [sbT(ph, "wst", [128, 8, 512]) for _ in range(2)]
                    wb = [sbT(ph, "wb", [128, 8, 512], BF16) for _ in range(2)]
                    pp = [psT(ph, "pp", [128, 512]) for _ in range(3)]
                    pt4 = [psT(ph, "pt4", [128, 4, 128], BF16) for _ in range(2)]
                    col = [sbT(ph, "col", [128, 4, 512], BF16) for _ in range(2)]
                    sq = sbT(ph, "sq", [128, 512])
                    ssgs = [sbT(ph, "ssg", [128, 16]) for _ in range(2)]
                    xn = sbT(ph, "xn", [128, 512])
                    xg = [sbT(ph, "xg", [128, 512]) for _ in range(2)]
                    xb = [sbT(ph, "xb", [128, 512], BF16) for _ in range(2)]
                    xk = sbT(ph, "xk", [128, 128], BF16)
                    tmp = [sbT(ph, "tmp", [128, 256]) for _ in range(4)]
                    rrt = [sbT(ph, "rrt", [128, 128]) for _ in range(2)]
                    vb = [sbT(ph, "vb", [128, 512], BF16) for _ in range(2)]
                    sg = sbT(ph, "sg", [128, 512])

                    def load_w(cg):
                        k = cg % 2
                        off, ncols = CGS[cg][1], CGS[cg][2]
                        DMA(wst[k][:, :, 0:ncols], w_in[l, :, off:off + ncols].rearrange("(c p) n -> p c n", p=128),
                            (), [f"wst{k}"])
                        CP("pool", wb[k][:, :, 0:ncols], wst[k][:, :, 0:ncols], [f"wst{k}"], [f"wb{k}"])

                    CGS = [("dq", 0, 512), ("dk", 512, 512), ("dv", 1024, 512), ("rq", 1536, 512), ("rk", 2048, 512),
                           ("rv", 2560, 512), ("rg", 3072, 512), ("gq", 3584, 512), ("gkv", 4096, 256)]
                    CGS += [("gate", 4352 + 512 * i, 512) for i in range(6)]

                    def rope(xgv1, xgv2, cv, sv, o1, o2, shape, rtags, otag, after=()):
                        n = 1
                        for s_ in shape[1:]:
                            n *= s_
                        pat_ = {2: "p (a b) -> p a b", 3: "p (a b c) -> p a b c"}[len(shape) - 1]
                        kw = {"a": shape[1], "b": shape[2]}
                        if len(shape) == 4:
                            kw["c"] = shape[3]
                        tv = [t_[:, 0:n].rearrange(pat_, **kw) for t_ in tmp]
                        TT("dve", tv[0], xgv1, cv, ALU.mult, rtags, ["tmp0"])
                        TT("dve", tv[1], xgv2, sv, ALU.mult, rtags, ["tmp1"])
                        TT("dve", o1, tv[0], tv[1], ALU.subtract, ["tmp0", "tmp1"] + list(after), [otag + "a"])
                        TT("pool", tv[2], xgv2, cv, ALU.mult, rtags, ["tmp2"])
                        TT("pool", tv[3], xgv1, sv, ALU.mult, rtags, ["tmp3"])
                        TT("pool", o2, tv[2], tv[3], ALU.add, ["tmp2", "tmp3"] + list(after), [otag + "b"])

                    def rmsA(ppv, ncols, G, gd, ppt, sl):
                        ssg = ssgs[sl]
                        ACT(sq[:, 0:ncols], ppv, AF.Square, [ppt], ["sq"])
                        RSUM("dve", ssg[:, 0:G], sq[:, 0:ncols].rearrange("p (g d) -> p g d", d=gd), ["sq"], [f"ssg{sl}"])
                        TS("dve", ssg[:, 0:G], ssg[:, 0:G], 1.0 / gd, EPS, ALU.mult, ALU.add, [f"ssg{sl}"], [f"ssg{sl}"])
                        TT("pool", ssg[:, 0:G], ssg[:, 0:G], mhalf[:, 0:G], ALU.pow, [f"ssg{sl}"], [f"ssg{sl}"])

                    def rms(ppv, ncols, G, gd, gain, ppt, xgt, sl):
                        ssg = ssgs[sl]
                        TT("dve", xn[:, 0:ncols].rearrange("p (g d) -> p g d", d=gd),
                           ppv.rearrange("p (g d) -> p g d", d=gd),
                           ssg[:, 0:G].unsqueeze(2).broadcast_to([128, G, gd]), ALU.mult, [ppt, f"ssg{sl}"], ["xn"])
                        TT("dve", xgt[0][:, 0:ncols].rearrange("p (g d) -> p g d", d=gd),
                           xn[:, 0:ncols].rearrange("p (g d) -> p g d", d=gd),
                           gain[:, :].unsqueeze(1).broadcast_to([128, G, gd]), ALU.mult, ["xn"], [xgt[1]])

                    def postA(kind, k3, sl):
                        ppk = pp[k3]
                        ppt = f"pp{k3}"
                        if kind in ("dq", "dk", "gq"):
                            rmsA(ppk[:, 0:512], 512, 8, 64, ppt, sl)
                        elif kind == "gkv":
                            rmsA(ppk[:, 0:128], 128, 2, 64, ppt, sl)

                    def post(cg, kind, ti, k, k3, sl):
                        ppk = pp[k3]
                        ppt = f"pp{k3}"
                        rows = slice(ti * 128, (ti + 1) * 128)
                        xgk, xbk = xg[k], xb[k]
                        xgt, xbt = f"xg{k}", f"xb{k}"
                        nblk = 0
                        if kind in ("dq", "dk"):
                            gain = g_dq if kind == "dq" else g_dk
                            rms(ppk[:, 0:512], 512, 8, 64, gain, ppt, (xgk, xgt), sl)
                            ACOPY(xbk[:], xgk[:], [xgt], [xbt])
                            xv = xgk[:, :].rearrange("p (g d) -> p g d", d=64)
                            ov = xbk[:, :].rearrange("p (g d) -> p g d", d=64)
                            cv = rpt[:, ti, 0:8].unsqueeze(1).broadcast_to([128, 8, 8])
                            sv = rpt[:, ti, 8:16].unsqueeze(1).broadcast_to([128, 8, 8])
                            rope(xv[:, :, 0:8], xv[:, :, 8:16], cv, sv, ov[:, :, 0:8], ov[:, :, 8:16],
                                 [128, 8, 8], [xgt, "rpt"], xbt, after=[xbt])
                            nblk = 4
                            dst = QA if kind == "dq" else None
                        elif kind in ("rq", "rk"):
                            r0 = 0 if kind == "rq" else 128
                            DMA(rrt[k][:], c_rr[rows, r0:r0 + 128], (), [f"rrt{k}"])
                            ACOPY(xgk[:], ppk[:], [ppt], [xgt])
                            xv = xgk[:, :].rearrange("p (g d) -> p g d", d=128)
                            ov = xbk[:, :].rearrange("p (g d) -> p g d", d=128)
                            cv = rrt[k][:, 0:64].unsqueeze(1).broadcast_to([128, 4, 64])
                            sv = rrt[k][:, 64:128].unsqueeze(1).broadcast_to([128, 4, 64])
                            rope(xv[:, :, 0:64], xv[:, :, 64:128], cv, sv, ov[:, :, 0:64], ov[:, :, 64:128],
                                 [128, 4, 64], [xgt, f"rrt{k}"], xbt)
                            nblk = 4
                            dst = RQT if kind == "rq" else RKT
                            if kind == "rk":
                                DMA(RKtok[rows, :], xbk[:], [xbt + "a", xbt + "b"], [])
                        elif kind == "gq":
                            rms(ppk[:, 0:512], 512, 8, 64, g_gq, ppt, (xgk, xgt), sl)
                            xv = xgk[:, :].rearrange("p (g a f d) -> p g a f d", a=2, f=2, d=16)
                            ov = xbk[:, :].rearrange("p (g a f d) -> p g a f d", a=2, f=2, d=16)
                            av = axt[:, ti, :].rearrange("p (cs a d) -> p cs a d", cs=2, a=2)
                            cv = av[:, 0, :, :].unsqueeze(1).broadcast_to([128, 8, 2, 16])
                            sv = av[:, 1, :, :].unsqueeze(1).broadcast_to([128, 8, 2, 16])
                            rope(xv[:, :, :, 0, :], xv[:, :, :, 1, :], cv, sv, ov[:, :, :, 0, :], ov[:, :, :, 1, :],
                                 [128, 8, 2, 16], [xgt, "axt"], xbt)
                            nblk = 4
                            dst = QC
                        elif kind == "gkv":
                            rms(ppk[:, 0:128], 128, 2, 64, g_gk, ppt, (xgk, xgt), sl)
                            xv = xgk[:, 0:128].rearrange("p (g a f d) -> p g a f d", a=2, f=2, d=16)
                            ov = xk[:, :].rearrange("p (g a f d) -> p g a f d", a=2, f=2, d=16)
                            av = axt[:, ti, :].rearrange("p (cs a d) -> p cs a d", cs=2, a=2)
                            cv = av[:, 0, :, :].unsqueeze(1).broadcast_to([128, 2, 2, 16])
                            sv = av[:, 1, :, :].unsqueeze(1).broadcast_to([128, 2, 2, 16])
                            rope(xv[:, :, :, 0, :], xv[:, :, :, 1, :], cv, sv, ov[:, :, :, 0, :], ov[:, :, :, 1, :],
                                 [128, 2, 2, 16], [xgt, "axt"], "xk")
                            CP("dve", xbk[:, 0:256].rearrange("p (g u d) -> p g u d", u=2, d=64),
                               xk[:, :].rearrange("p (g d) -> p g d", d=64).unsqueeze(2).broadcast_to([128, 2, 2, 64]),
                               ["xka", "xkb"], [xbt])
                            ACOPY(vb[k][:, 0:128], ppk[:, 128:256], [ppt], [f"vb{k}"])
                            DMA(VC.ap()[rows, :], vb[k][:, 0:128], [f"vb{k}"], [])
                            nblk = 2
                            dst = KC.ap()
                        elif kind in ("dv", "rv"):
                            ACOPY(vb[k][:], ppk[:], [ppt], [f"vb{k}"])
                            if kind == "dv":
                                DMA(VAi[ti // 16].ap()[(ti % 16) * 128:(ti % 16 + 1) * 128, :], vb[k][:], [f"vb{k}"], [])
                            else:
                                DMA(RVtok[rows, :], vb[k][:], [f"vb{k}"], [])
                        elif kind == "rg":
                            ACT(sg[:], ppk[:], AF.Sigmoid, [ppt], ["sg"])
                            TT("dve", vb[k][:], sg[:], ppk[:], ALU.mult, ["sg", ppt], [f"vb{k}"])
                            DMA(RG[rows, :], vb[k][:], [f"vb{k}"], [])
                        else:
                            gi = cg - 9
                            ACT(vb[k][:], ppk[:], AF.Sigmoid, [ppt], [f"vb{k}"])
                            DMA(GATES[rows, gi * 512:(gi + 1) * 512], vb[k][:], [f"vb{k}"], [])
                        if nblk:
                            ck = (ti // 4) % 2
                            for b_ in range(nblk):
                                TR(pt4[k][:, b_, :], xbk[:, b_ * 128:(b_ + 1) * 128], ident[:], [xbt, xbt + "a", xbt + "b"], [f"pt4{k}"])
                            ACOPY(col[ck][:, 0:nblk, (ti % 4) * 128:(ti % 4 + 1) * 128], pt4[k][:, 0:nblk, :],
                                  [f"pt4{k}"], [f"col{ck}"])
                            if ti % 4 == 3:
                                tb = ti // 4
                                if dst is None:
                                    for i2 in range(2):
                                        DMA(KAi[i2].ap()[:, tb * 512:(tb + 1) * 512].rearrange("(b p) t -> p b t", p=128),
                                            col[ck][:, 2 * i2:2 * i2 + 2, :], [f"col{ck}"], [])
                                else:
                                    DMA(dst[0:nblk * 128, tb * 512:(tb + 1) * 512].rearrange("(b p) t -> p b t", p=128),
                                        col[ck][:, 0:nblk, :], [f"col{ck}"], [])

                    load_w(0)
                    it = 0
                    hist = []
                    for cg in range(15):
                        kind, off, ncols = CGS[cg]
                        if cg + 1 < 15:
                            load_w(cg + 1)
                        wk = cg % 2
                        for ti in range(NT):
                            k = it % 2
                            k3 = it % 3
                            it += 1
                            ppk = pp[k3]
                            ppt = f"pp{k3}"
                            for c in range(8):
                                MM(ppk[:, 0:ncols], uT[:, c, ti * 128:(ti + 1) * 128], wb[wk][:, c, 0:ncols],
                                   c == 0, c == 7, ["uT", f"wb{wk}"], [ppt])
                            hist.append((cg, kind, ti, k, k3, k))
                            if len(hist) >= 2:
                                postA(hist[-2][1], hist[-2][4], hist[-2][5])
                            if len(hist) >= 3:
                                post(*hist[-3])
                    postA(hist[-1][1], hist[-1][4], hist[-1][5])
                    post(*hist[-2])
                    post(*hist[-1])
                    S.barrier()

                for ph in phase_ctx(2):
                    kt = [sbT(ph, "kt", [128, NT, 128], BF16) for _ in range(2)]
                    vt = [sbT(ph, "vt", [128, NT, 128], BF16) for _ in range(2)]
                    vf = sbT(ph, "vf", [128, NT, 128], BF16)
                    vbk = sbT(ph, "vbk", [128, NT, 128], BF16)
                    wfb = sbT(ph, "wfb", [128, 2, NT])
                    psf = psT(ph, "psf", [128, 128])
                    psb = psT(ph, "psb", [128, 128])
                    sst = [sbT(ph, "sst", [128, 2, 128]) for _ in range(2)]
                    for h in range(4):
                        k = h % 2
                        ktg = DMA3(kt[k][:], RKtok[:, h * 128:(h + 1) * 128].rearrange("(n p) d -> p n d", p=128), (), [f"kt{k}"])
                        vtg = DMA3(vt[k][:], RVtok[:, h * 128:(h + 1) * 128].rearrange("(n p) d -> p n d", p=128), (), [f"vt{k}"])
                        ACT(wfb[:, 0, :], T1, AF.Exp, [], ["wf"], scale=lfb[:, h:h + 1])
                        ACT(wfb[:, 1, :], T2, AF.Exp, [], ["wb_"], scale=lfb[:, 4 + h:5 + h])
                        TT("dve", vf[:], vt[k][:], wfb[:, 0, :].unsqueeze(2).broadcast_to([128, NT, 128]), ALU.mult,
                           vtg + ["wf"], ["vf"])
                        TT("pool", vbk[:], vt[k][:], wfb[:, 1, :].unsqueeze(2).broadcast_to([128, NT, 128]), ALU.mult,
                           vtg + ["wb_"], ["vbk"])
                        for n in range(NT):
                            MM(psf[:], kt[k][:, n, :], vf[:, n, :], n == 0, n == NT - 1, ktg + ["vf"], ["psf"])
                        for n in range(NT):
                            MM(psb[:], kt[k][:, n, :], vbk[:, n, :], n == 0, n == NT - 1, ktg + ["vbk"], ["psb"])
                        ACOPY(sst[k][:, 0, :], psf[:], ["psf"], [f"sstf{k}"])
                        CP("dve", sst[k][:, 1, :], psb[:], ["psb"], [f"sstb{k}"])
                        DMA(ST.ap()[h * 128:(h + 1) * 128, :], sst[k][:, 0, :], [f"sstf{k}"], [])
                        DMA(ST.ap()[512 + h * 128:512 + (h + 1) * 128, :], sst[k][:, 1, :], [f"sstb{k}"], [])
                    S.barrier()
                    CC(KAi[0], KAoi[0], (), ["cc"])
                    CC(KAi[1], KAoi[1], ["cc"], ["cc"])
                    CC(KC, KCo, ["cc"], ["cc"])
                    CC(VAi[0], VAoi[0], ["cc"], ["cc"])
                    CC(VAi[1], VAoi[1], ["cc"], ["cc"])
                    CC(VC, VCo, ["cc"], ["cc"])
                    CC(ST, STo, ["cc"], ["cc"])
                    S.op("pool", lambda e: e.nop(), ["cc"], ["cc2"])
                    S.barrier()
                if dbg and l == 0 and stop >= 2:
                    DMA(dbg_out["d_STo"], STo.ap(), (), [])
                    S.barrier()

                for ph in phase_ctx(3):
                    kT = [sbT(ph, "kT", [128, SEQ], BF16) for _ in range(2)]
                    vv = [sbT(ph, "vv", [128, 64, 128], BF16) for _ in range(2)]
                    qT = [sbT(ph, "qT", [128, T], BF16) for _ in range(2)]
                    Eb = [sbT(ph, "Eb", [128, 1024], BF16) for _ in range(3)]
                    Es = [sbT(ph, "Es", [128, 1024], BF16) for _ in range(2)]
                    sc = [psT(ph, "sc", [128, 1024]) for _ in range(2)]
                    po = [psT(ph, "po", [128, 512]) for _ in range(2)]
                    pS = [psT(ph, "pS", [128, 512]) for _ in range(2)]
                    rs = [sbT(ph, "rs", [128, 512]) for _ in range(2)]
                    on = [sbT(ph, "on", [128, 512]) for _ in range(2)]
                    dd = sbT(ph, "dd", [128, 512])
                    dsq = sbT(ph, "dsq", [128, 512])
                    rstd = sbT(ph, "rstd", [128, 512])
                    ao = [sbT(ph, "ao", [128, 512], BF16) for _ in range(2)]

                    def loadA(h):
                        k = h % 2
                        for r in range(2):
                            DMA(kT[k][:, r * T:(r + 1) * T],
                                KAoi[h // 2].ap()[r * 256 + (h % 2) * 128:r * 256 + (h % 2 + 1) * 128, :], (), [f"kT{k}{r}"])
                            for i2 in range(2):
                                DMA3(vv[k][:, r * 32 + i2 * 16:r * 32 + (i2 + 1) * 16, :],
                                     VAoi[i2].ap()[r * 2048:(r + 1) * 2048, h * 128:(h + 1) * 128].rearrange("(n p) d -> p n d", p=128),
                                     (), [f"vv{k}{r}{i2}"])
                        DMA(qT[k][:], QA[h * 128:(h + 1) * 128, :], (), [f"qT{k}"])

                    loadA(0)
                    ecnt = 0
                    scnt = 0
                    for h in range(4):
                        k = h % 2
                        if h + 1 < 4:
                            loadA(h + 1)
                        ktags = [f"kT{k}0", f"kT{k}1"]
                        vtags = [[f"vv{k}{r}{i2}_{a}" for i2 in range(2) for a in range(0, 16, 8)] for r in range(2)]
                        for qb in range(8):
                            qs = slice(qb * 512, (qb + 1) * 512)

                            def QK(kc):
                                for m in range(2):
                                    MM(sc[kc % 2][:, m * 512:(m + 1) * 512], kT[k][64 * m:64 * m + 64, kc * 128:(kc + 1) * 128],
                                       qT[k][64 * m:64 * m + 64, qs], True, True, [ktags[kc // 32], f"qT{k}"], [f"sc{kc % 2}"])

                            def SUMMM(sb_, first, last):
                                for m in range(2):
                                    MM(pS[m][:], ones_bf[:], Es[sb_][:, m * 512:(m + 1) * 512], first, last,
                                       [f"Es{sb_}"], [f"pS{m}"])

                            pend_sum = None
                            QK(0)
                            QK(1)
                            for kc in range(64):
                                e3 = kc % 3
                                ACT(Eb[e3][:], sc[kc % 2][:], AF.Exp, [f"sc{kc % 2}"], [f"E{e3}"], scale=0.125)
                                if kc + 2 < 64:
                                    QK(kc + 2)
                                for m in range(2):
                                    MM(po[m][:], vv[k][:, kc, :], Eb[e3][:, m * 512:(m + 1) * 512], kc == 0, kc == 63,
                                       vtags[kc // 32] + [f"E{e3}"], [f"po{m}"])
                                if kc % 2 == 1:
                                    if pend_sum is not None:
                                        SUMMM(*pend_sum)
                                    sb_ = scnt % 2
                                    scnt += 1
                                    TT("dve", Es[sb_][:], Eb[(kc - 1) % 3][:], Eb[e3][:], ALU.add,
                                       [f"E{(kc - 1) % 3}", f"E{e3}"], [f"Es{sb_}"])
                                    pend_sum = (sb_, kc == 1, kc == 63)
                            SUMMM(*pend_sum)
                            for m in range(2):
                                RECIP(rs[m][:], pS[m][:], [f"pS{m}"], [f"rs{m}"])
                                TT("dve", on[m][:], po[m][:], rs[m][:], ALU.mult, [f"po{m}", f"rs{m}"], [f"on{m}"])
                            STT("dve", dd[:], on[1][:], nlam[:, 0:1], on[0][:], ALU.mult, ALU.add, ["on0", "on1"], ["dd"])
                            TT("pool", dsq[:], dd[:], dd[:], ALU.mult, ["dd"], ["dsq"])
                            MM(pS[0][:], ones_f[:], dsq[:], True, True, ["dsq"], ["pS0"])
                            ACT(rstd[:], pS[0][:], AF.Ln, ["pS0"], ["rstd"], scale=1.0 / 128, bias=epsb[:, 0:1])
                            ACT(rstd[:], rstd[:], AF.Exp, ["rstd"], ["rstd2"], scale=-0.5)
                            a_ = ao[ecnt % 2]
                            at = f"ao{ecnt % 2}"
                            ecnt += 1
                            STT("dve", a_[:], dd[:], subc[:, 1:2], rstd[:], ALU.mult, ALU.mult, ["dd", "rstd2"], [at])
                            DMA(BR[0, h * 128:(h + 1) * 128, qs], a_[:], [at], [])
                    S.barrier()

                for ph in phase_ctx(4):
                    kT = [sbT(ph, "kTc", [128, SEQ], BF16) for _ in range(2)]
                    vv = [sbT(ph, "vvc", [128, 64, 128], BF16) for _ in range(2)]
                    qT = [sbT(ph, "qTc", [128, T], BF16) for _ in range(2)]
                    Eb = [sbT(ph, "Ebc", [128, 1024], BF16) for _ in range(3)]
                    sc = [psT(ph, "scc", [128, 1024]) for _ in range(2)]
                    po = [psT(ph, "poc", [128, 512]) for _ in range(2)]
                    pb = psT(ph, "pbc", [128, 512])
                    oc = [sbT(ph, "oc", [128, 512]) for _ in range(2)]
                    rs = [sbT(ph, "rsc", [128, 512]) for _ in range(2)]
                    co = [sbT(ph, "co", [128, 512], BF16) for _ in range(2)]
                    for k in range(2):
                        MEMSET("dve", vv[k][:, :, 64:128], 1.0, [f"vone{k}"])

                    def loadC(p):
                        k = p % 2
                        g = p // 2
                        for r in range(2):
                            DMA(kT[k][:, r * T:(r + 1) * T], KCo.ap()[r * 256 + g * 128:r * 256 + (g + 1) * 128, :], (), [f"kT{k}{r}"])
                            DMA3(vv[k][:, r * 32:(r + 1) * 32, 0:64],
                                 VCo.ap()[r * T:(r + 1) * T, g * 64:(g + 1) * 64].rearrange("(n p) d -> p n d", p=128),
                                 (), [f"vv{k}{r}"])
                        DMA(qT[k][:], QC[p * 128:(p + 1) * 128, :], (), [f"qT{k}"])

                    loadC(0)
                    ecnt = 0
                    for p in range(4):
                        k = p % 2
                        if p + 1 < 4:
                            loadC(p + 1)
                        ktags = [f"kT{k}0", f"kT{k}1"]
                        vtags = [[f"vv{k}{r}_{a}" for a in range(0, 32, 8)] for r in range(2)]
                        for qb in range(8):
                            qs = slice(qb * 512, (qb + 1) * 512)

                            def QKc(kc):
                                for j in range(2):
                                    MM(sc[kc % 2][:, j * 512:(j + 1) * 512], kT[k][64 * j:64 * j + 64, kc * 128:(kc + 1) * 128],
                                       qT[k][64 * j:64 * j + 64, qs], True, True, [ktags[kc // 32], f"qT{k}"], [f"sc{kc % 2}"])

                            QKc(0)
                            QKc(1)
                            for kc in range(64):
                                e3 = kc % 3
                                ACT(Eb[e3][:], sc[kc % 2][:], AF.Exp, [f"sc{kc % 2}"], [f"E{e3}"], scale=0.125)
                                if kc + 2 < 64:
                                    QKc(kc + 2)
                                for j in range(2):
                                    MM(po[j][:], vv[k][:, kc, :], Eb[e3][:, j * 512:(j + 1) * 512], kc == 0, kc == 63,
                                       vtags[kc // 32] + [f"vone{k}", f"E{e3}"], [f"po{j}"])
                            for j in range(2):
                                CP("dve", oc[j][:], po[j][:], [f"po{j}"], [f"oc{j}"])
                                MM(pb[0:64, :], SEL[:, :], oc[j][:], True, True, [f"oc{j}"], ["pb"])
                                RECIP(rs[j][0:64, :], pb[0:64, :], ["pb"], [f"rs{j}"])
                                c_ = co[ecnt % 2]
                                ct = f"co{ecnt % 2}"
                                ecnt += 1
                                TT("dve", c_[0:64, :], oc[j][0:64, :], rs[j][0:64, :], ALU.mult, [f"oc{j}", f"rs{j}"], [ct])
                                DMA(BR[2, p * 128 + j * 64:p * 128 + (j + 1) * 64, qs], c_[0:64, :], [ct], [])
                    S.barrier()

                for ph in phase_ctx(5):
                    qTr = [sbT(ph, "qTr", [128, T], BF16) for _ in range(2)]
                    kTr = [sbT(ph, "kTr", [128, T], BF16) for _ in range(2)]
                    kt = [sbT(ph, "ktr", [128, NT, 128], BF16) for _ in range(2)]
                    vt = [sbT(ph, "vtr", [128, NT, 128], BF16) for _ in range(2)]
                    gt = [sbT(ph, "gtr", [128, NT, 128], BF16) for _ in range(2)]
                    vzf = sbT(ph, "vzf", [128, NT, 128], BF16)
                    vzb = sbT(ph, "vzb", [128, NT, 128], BF16)
                    qxf = sbT(ph, "qxf", [128, NT, 128], BF16)
                    qxb = sbT(ph, "qxb", [128, NT, 128], BF16)
                    SB = sbT(ph, "SBst", [128, NT, 128], BF16)
                    Dm = sbT(ph, "Dm", [128, 128])
                    dt1 = sbT(ph, "dt1", [128, 128])
                    Xf = sbT(ph, "Xf", [128, 128])
                    Xb = sbT(ph, "Xb", [128, 128])
                    zc = sbT(ph, "zc", [128, 4])
                    Fst = sbT(ph, "Fst", [128, 128])
                    Bst = sbT(ph, "Bst", [128, 128])
                    Fbf = [sbT(ph, "Fbf", [128, 128], BF16) for _ in range(2)]
                    sti = sbT(ph, "sti", [128, 2, 128])
                    scD = [sbT(ph, "scD", [128, 128], BF16) for _ in range(2)]
                    gs = sbT(ph, "gs", [128, 8])
                    xc = sbT(ph, "xc", [128, 128])
                    jk = sbT(ph, "jk", [128, 128])
                    yt = sbT(ph, "yt", [128, 128])
                    rb = [sbT(ph, "rb", [128, 128], BF16) for _ in range(2)]
                    roT = [sbT(ph, "roT", [128, 512], BF16) for _ in range(2)]
                    pkv = [psT(ph, "pkv", [128, 128]) for _ in range(2)]
                    psc = [psT(ph, "psc", [128, 128]) for _ in range(2)]
                    pov = [psT(ph, "pov", [128, 128]) for _ in range(2)]
                    ptt = [psT(ph, "ptt", [128, 128], BF16) for _ in range(2)]

                    def loadR(h):
                        k = h % 2
                        DMA(qTr[k][:], RQT[h * 128:(h + 1) * 128, :], (), [f"qTr{k}"])
                        DMA(kTr[k][:], RKT[h * 128:(h + 1) * 128, :], (), [f"kTr{k}"])
                        DMA3(kt[k][:], RKtok[:, h * 128:(h + 1) * 128].rearrange("(n p) d -> p n d", p=128), (), [f"kt{k}"])
                        DMA3(vt[k][:], RVtok[:, h * 128:(h + 1) * 128].rearrange("(n p) d -> p n d", p=128), (), [f"vt{k}"])
                        DMA3(gt[k][:], RG[:, h * 128:(h + 1) * 128].rearrange("(n p) d -> p n d", p=128), (), [f"gt{k}"])

                    loadR(0)
                    cnt = 0
                    for h in range(4):
                        k = h % 2
                        if h + 1 < 4:
                            loadR(h + 1)
                        lf = lfb[:, h:h + 1]
                        lb = lfb[:, 4 + h:5 + h]
                        ktg = [f"kt{k}_{a_}" for a_ in range(0, 32, 8)]
                        vtg = [f"vt{k}_{a_}" for a_ in range(0, 32, 8)]
                        gtg = [f"gt{k}_{a_}" for a_ in range(0, 32, 8)]
                        TS("dve", dt1[:], P1, lf, None, ALU.mult, None, [], ["dt1"])
                        STT("dve", dt1[:], P2, lb, dt1[:], ALU.mult, ALU.add, ["dt1"], ["dt1"])
                        ACT(Dm[:], dt1[:], AF.Exp, ["dt1"], ["Dm"])
                        ACT(Xf[:], IF1, AF.Exp, [], ["Xf"], scale=lf)
                        ACT(Xb[:], IF2, AF.Exp, [], ["Xb"], scale=lb)
                        ACT(zc[:, 0:1], ZF, AF.Exp, [], ["zc0"], scale=lf)
                        ACT(zc[:, 1:2], ZB, AF.Exp, [], ["zc1"], scale=lb)
                        ACT(zc[:, 2:3], lf, AF.Exp, [], ["zc2"], scale=128.0)
                        ACT(zc[:, 3:4], lb, AF.Exp, [], ["zc3"], scale=128.0)
                        TS("dve", vzf[:], vt[k][:], zc[:, 0:1], None, ALU.mult, None, vtg + ["zc0"], ["vzf"])
                        TT("pool", vzb[:], vt[k][:], zc[:, 1:2].unsqueeze(2).broadcast_to([128, NT, 128]), ALU.mult, vtg + ["zc1"], ["vzb"])
                        TT("dve", qxf[:], qTr[k][:, :].rearrange("p (n i) -> p n i", i=128),
                           Xf[:, :].unsqueeze(1).broadcast_to([128, NT, 128]), ALU.mult, [f"qTr{k}", "Xf"], ["qxf"])
                        TT("pool", qxb[:], qTr[k][:, :].rearrange("p (n i) -> p n i", i=128),
                           Xb[:, :].unsqueeze(1).broadcast_to([128, NT, 128]), ALU.mult, [f"qTr{k}", "Xb"], ["qxb"])
                        DMA(sti[:, 0, :], STo.ap()[h * 128:(h + 1) * 128, :], (), ["sti0"])
                        DMA(sti[:, 1, :], STo.ap()[1024 + 512 + h * 128:1024 + 512 + (h + 1) * 128, :], (), ["sti1"])
                        TS("dve", Fst[:], sti[:, 0, :], flags[:, 0:1], None, ALU.mult, None, ["sti0"], ["Fst"])
                        TS("dve", Bst[:], sti[:, 1, :], flags[:, 1:2], None, ALU.mult, None, ["sti1"], ["Bst"])
                        for n in range(NT - 1, -1, -1):
                            pk = pkv[cnt % 2]
                            pkt = f"pkv{cnt % 2}"
                            cnt += 1
                            ACOPY(SB[:, n, :], Bst[:], ["Bst"], ["SB"])
                            if n > 0:
                                MM(pk[:], kt[k][:, n, :], vzb[:, n, :], True, True, ktg + ["vzb"], [pkt])
                                STT("dve", Bst[:], Bst[:], zc[:, 3:4], pk[:], ALU.mult, ALU.add, [pkt, "zc3", "Bst"], ["Bst"])
                        for n in range(NT):
                            ns = slice(n * 128, (n + 1) * 128)
                            b2 = n % 2
                            CP("pool", Fbf[b2][:], Fst[:], ["Fst"], [f"Fbf{b2}"])
                            MM(psc[b2][:], kTr[k][:, ns], qTr[k][:, ns], True, True, [f"kTr{k}", f"qTr{k}"], [f"psc{b2}"])
                            TT("dve", scD[b2][:], psc[b2][:], Dm[:], ALU.mult, [f"psc{b2}", "Dm"], [f"scD{b2}"])
                            MM(pov[b2][:], scD[b2][:], vt[k][:, n, :], True, False, [f"scD{b2}"] + vtg, [f"pov{b2}"])
                            MM(pov[b2][:], qxf[:, n, :], Fbf[b2][:], False, False, ["qxf", f"Fbf{b2}"], [f"pov{b2}"])
                            MM(pov[b2][:], qxb[:, n, :], SB[:, n, :], False, True, ["qxb", "SB"], [f"pov{b2}"])
                            if n < NT - 1:
                                pk = pkv[cnt % 2]
                                pkt = f"pkv{cnt % 2}"
                                cnt += 1
                                MM(pk[:], kt[k][:, n, :], vzf[:, n, :], True, True, ktg + ["vzf"], [pkt])
                                STT("dve", Fst[:], Fst[:], zc[:, 2:3], pk[:], ALU.mult, ALU.add, [pkt, "zc2", "Fst"], ["Fst"])
                            RSUM("dve", gs[:, 0:1], pov[b2][:], [f"pov{b2}"], ["gs0"])
                            TS("dve", gs[:, 1:2], gs[:, 0:1], -1.0 / 128, None, ALU.mult, None, ["gs0"], ["gs1"])
                            TS("dve", xc[:], pov[b2][:], gs[:, 1:2], None, ALU.add, None, [f"pov{b2}", "gs1"], ["xc"])
                            ACT(jk[:], xc[:], AF.Square, ["xc"], ["jk", "gs2"], accum=gs[:, 2:3])
                            TS("dve", gs[:, 3:4], gs[:, 2:3], 1.0 / 128, 1e-5, ALU.mult, ALU.add, ["gs2"], ["gs3"])
                            TT("pool", gs[:, 3:4], gs[:, 3:4], mhalf[:, 0:1], ALU.pow, ["gs3"], ["gs4"])
                            STT("dve", yt[:], xc[:], gs[:, 3:4], rgn[:, h * 128:(h + 1) * 128], ALU.mult, ALU.mult,
                                ["xc", "gs4"], ["yt"])
                            TT("pool", rb[b2][:], yt[:], gt[k][:, n, :], ALU.mult, ["yt"] + gtg, [f"rb{b2}"])
                            TR(ptt[b2][:], rb[b2][:], ident[:], [f"rb{b2}"], [f"ptt{b2}"])
                            rk_ = (n // 4) % 2
                            ACOPY(roT[rk_][:, (n % 4) * 128:(n % 4 + 1) * 128], ptt[b2][:], [f"ptt{b2}"], [f"roT{rk_}"])
                            if n % 4 == 3:
                                tb = n // 4
                                DMA(BR[1, h * 128:(h + 1) * 128, tb * 512:(tb + 1) * 512], roT[rk_][:], [f"roT{rk_}"], [])
                    S.barrier()
                if dbg and l == 0 and stop >= 5:
                    DMA(dbg_out["d_BR"], BR, (), [])
                    S.barrier()

                for ph in phase_ctx(6):
                    Wb = sbT(ph, "Wb", [128, 12, D], BF16)
                    Wo = sbT(ph, "Wo", [128, 8, D], BF16)
                    wst = [sbT(ph, "wstd", [128, 4, D]) for _ in range(2)]
                    for i in range(3):
                        k = i % 2
                        DMA(wst[k][:], w_branch[l, i, :, :].rearrange("(c p) n -> p c n", p=128), (), [f"wst{k}"])
                        CP("pool", Wb[:, 4 * i:4 * i + 4, :], wst[k][:], [f"wst{k}"], ["Wb"])
                    for i in range(2):
                        k = (3 + i) % 2
                        DMA(wst[k][:], w_out[l, i * 512:(i + 1) * 512, :].rearrange("(c p) n -> p c n", p=128), (), [f"wst{k}"])
                        CP("pool", Wo[:, 4 * i:4 * i + 4, :], wst[k][:], [f"wst{k}"], ["Wo"])
                    brT = [sbT(ph, "brT", [128, 12, 512], BF16) for _ in range(2)]
                    gtile = [sbT(ph, "gtile", [128, 3072], BF16) for _ in range(2)]
                    ht = [sbT(ph, "htd", [128, D]) for _ in range(2)]
                    mg = sbT(ph, "mg", [128, D])
                    mt = sbT(ph, "mt", [128, D])
                    mgb = sbT(ph, "mgb", [128, D], BF16)
                    mT = sbT(ph, "mT", [128, 8, 128], BF16)
                    h1 = [sbT(ph, "h1", [128, D]) for _ in range(2)]
                    junk = sbT(ph, "junkd", [128, D], BF16)
                    ss1 = sbT(ph, "ss1d", [128, 2])
                    u2 = sbT(ph, "u2", [128, D], BF16)
                    u2T = [sbT(ph, "u2T", [128, 8, 512], BF16) for _ in range(2)]
                    pj = [psT(ph, "pj", [128, 512]) for _ in range(6)]
                    ptr = [psT(ph, "ptrd", [128, 8, 128], BF16) for _ in range(2)]
                    for tb in range(8):
                        bk = tb % 2
                        DMA(brT[bk][:], BR[:, :, tb * 512:(tb + 1) * 512].rearrange("n (c p) t -> p (n c) t", p=128),
                            (), [f"brT{bk}"])
                        for t4 in range(4):
                            ti = tb * 4 + t4
                            k = ti % 2
                            rows = slice(ti * 128, (ti + 1) * 128)
                            ts_ = slice(t4 * 128, (t4 + 1) * 128)
                            DMA(gtile[k][:], GATES[rows, :], (), [f"gtile{k}"])
                            DMA(ht[k][:], Hsrc[rows, :], (), [f"htd{k}"])
                            for n in range(3):
                                for hf in range(2):
                                    pi_ = (2 * n + hf) % 4
                                    pjk = pj[pi_]
                                    for c in range(4):
                                        MM(pjk[:], brT[bk][:, 4 * n + c, ts_], Wb[:, 4 * n + c, hf * 512:(hf + 1) * 512],
                                           c == 0, c == 3, [f"brT{bk}", "Wb"], [f"pj{pi_}"])
                                    hs = slice(hf * 512, (hf + 1) * 512)
                                    gsl = slice(n * D + hf * 512, n * D + (hf + 1) * 512)
                                    if n == 0:
                                        TT("dve", mg[:, hs], pjk[:], gtile[k][:, gsl], ALU.mult, [f"pj{pi_}", f"gtile{k}"], [f"mg{hf}"])
                                    else:
                                        TT("dve", mt[:, hs], pjk[:], gtile[k][:, gsl], ALU.mult, [f"pj{pi_}", f"gtile{k}"], [f"mt{hf}"])
                                        TT("pool", mg[:, hs], mg[:, hs], mt[:, hs], ALU.add, [f"mg{hf}", f"mt{hf}"], [f"mg{hf}"])
                            ACOPY(mgb[:], mg[:], ["mg0", "mg1"], ["mgb"])
                            for c in range(8):
                                TR(ptr[0][:, c, :], mgb[:, c * 128:(c + 1) * 128], ident[:], ["mgb"], ["ptr0"])
                            ACOPY(mT[:], ptr[0][:], ["ptr0"], ["mT"])
                            for hf in range(2):
                                pjk = pj[4 + hf]
                                for c in range(8):
                                    MM(pjk[:], mT[:, c, :], Wo[:, c, hf * 512:(hf + 1) * 512], c == 0, c == 7, ["mT", "Wo"], [f"pj{4 + hf}"])
                                hs = slice(hf * 512, (hf + 1) * 512)
                                TT("dve", h1[k][:, hs], pjk[:], ht[k][:, hs], ALU.add, [f"pj{4 + hf}", f"htd{k}"], [f"h1{k}{hf}"])
                            DMA(H1[rows, :], h1[k][:], [f"h1{k}0", f"h1{k}1"], [f"h1st{k}"])
                            ACT(junk[:], h1[k][:], AF.Square, [f"h1{k}0", f"h1{k}1"], ["junk", "ssa"], accum=ss1[:, 0:1])
                            TS("dve", ss1[:, 1:2], ss1[:, 0:1], 1.0 / D, EPS, ALU.mult, ALU.add, ["ssa"], ["ssb"])
                            TT("pool", ss1[:, 1:2], ss1[:, 1:2], mhalf[:, 0:1], ALU.pow, ["ssb"], ["ssc"])
                            STT("dve", u2[:], h1[k][:], ss1[:, 1:2], g_mlp[:], ALU.mult, ALU.mult,
                                [f"h1{k}0", f"h1{k}1", "ssc"], ["u2"])
                            for c in range(8):
                                TR(ptr[1][:, c, :], u2[:, c * 128:(c + 1) * 128], ident[:], ["u2"], ["ptr1"])
                            ACOPY(u2T[bk][:, :, ts_], ptr[1][:], ["ptr1"], [f"u2T{bk}"])
                        DMA(U2T[:, tb * 512:(tb + 1) * 512].rearrange("(c p) t -> p c t", p=128), u2T[bk][:], [f"u2T{bk}"], [f"u2Tst{bk}"])
                    S.barrier()
                if dbg and l == 0 and stop >= 6:
                    DMA(dbg_out["d_H1"], H1, (), [])
                    S.barrier()

                for ph in phase_ctx(7):
                    W1 = sbT(ph, "W1", [128, 8, DFF], BF16)
                    W2 = sbT(ph, "W2", [128, 32, D], BF16)
                    wst = [sbT(ph, "wste", [128, 2, D]) for _ in range(2)]
                    wi = 0
                    for c in range(8):
                        for hh in range(2):
                            k = wi % 2
                            wi += 1
                            DMA(wst[k][:, :, :].rearrange("p a b -> p (a b)"),
                                w_mlp_in[l, c * 128:(c + 1) * 128, hh * 2048:(hh + 1) * 2048], (), [f"wst{k}"])
                            CP("pool", W1[:, c, hh * 2048:(hh + 1) * 2048], wst[k][:, :, :].rearrange("p a b -> p (a b)"),
                               [f"wst{k}"], ["W1"])
                    for i in range(16):
                        k = wi % 2
                        wi += 1
                        DMA(wst[k][:], w_mlp_out[l, i * 256:(i + 1) * 256, :].rearrange("(c p) n -> p c n", p=128), (), [f"wst{k}"])
                        CP("pool", W2[:, 2 * i:2 * i + 2, :], wst[k][:], [f"wst{k}"], ["W2"])
                    TB = 256
                    u2T = [sbT(ph, "u2Te", [128, 8, TB], BF16) for _ in range(2)]
                    mTt = sbT(ph, "mTt", [128, 32, TB], BF16)
                    rl = [sbT(ph, "rl", [128, TB], BF16) for _ in range(2)]
                    h1 = [sbT(ph, "h1e", [128, D]) for _ in range(2)]
                    pm = [psT(ph, "pm", [128, 512]) for _ in range(2)]
                    pq = [psT(ph, "pq", [128, 512]) for _ in range(4)]

                    def loadU(tb):
                        bk = tb % 2
                        DMA(u2T[bk][:], U2T[:, tb * TB:(tb + 1) * TB].rearrange("(c p) t -> p c t", p=128), (), [f"u2T{bk}"])

                    loadU(0)
                    for tb in range(T // TB):
                        bk = tb % 2
                        if tb + 1 < T // TB:
                            loadU(tb + 1)
                        for fc in range(32):
                            k = fc % 2
                            for c in range(8):
                                MM(pm[k][:, 0:TB], W1[:, c, fc * 128:(fc + 1) * 128], u2T[bk][:, c, :], c == 0, c == 7,
                                   ["W1", f"u2T{bk}"], [f"pm{k}"])
                            ACT(rl[k][:], pm[k][:, 0:TB], AF.Relu, [f"pm{k}"], [f"rl{k}"])
                            TT("dve", mTt[:, fc, :], rl[k][:], rl[k][:], ALU.mult, [f"rl{k}"], ["mTt"])
                        for t4 in range(TB // 128):
                            ti = tb * (TB // 128) + t4
                            k = ti % 2
                            rows = slice(ti * 128, (ti + 1) * 128)
                            DMA(h1[k][:], H1[rows, :], (), [f"h1e{k}", f"ho{k}0", f"ho{k}1"])
                            for hf in range(2):
                                pqk = pq[2 * k + hf]
                                for fc in range(32):
                                    MM(pqk[:], mTt[:, fc, t4 * 128:(t4 + 1) * 128], W2[:, fc, hf * 512:(hf + 1) * 512],
                                       fc == 0, fc == 31, ["mTt", "W2"], [f"pq{2 * k + hf}"])
                                hs = slice(hf * 512, (hf + 1) * 512)
                                TT("dve", h1[k][:, hs], pqk[:], h1[k][:, hs], ALU.add, [f"pq{2 * k + hf}", f"h1e{k}"], [f"ho{k}{hf}"])
                            DMA(Hdst[rows, :], h1[k][:], [f"ho{k}0", f"ho{k}1"], [f"host{k}"])
                    S.barrier()
        S.emit(top)
    return nc


def _rope_tab(pos, dim, theta):
    inv = theta ** (-np.arange(0, dim, 2, dtype=np.float32) / np.float32(dim))
    ang = pos.astype(np.float32)[:, None] * inv[None, :].astype(np.float32)
    return np.cos(ang).astype(np.float32), np.sin(ang).astype(np.float32)


def _consts(half):
    pos = np.arange(half * T, (half + 1) * T, dtype=np.float32)
    c, s = _rope_tab(pos, 16, 500000.0)
    c_rp = np.concatenate([c, s], axis=1).astype(np.float32)
    c, s = _rope_tab(pos, 128, 10000.0)
    sc = np.float32(128 ** -0.5)
    c_rr = np.concatenate([c, s, c * sc, s * sc], axis=1).astype(np.float32)
    rc, rs_ = _rope_tab(np.floor(pos / 64), 32, 10000.0)
    cc, cs = _rope_tab(np.mod(pos, 64), 32, 10000.0)
    c_ax = np.concatenate([rc, cc, rs_, cs], axis=1).astype(np.float32)
    j = np.arange(128, dtype=np.float32)[:, None]
    i = np.arange(128, dtype=np.float32)[None, :]
    P1 = np.maximum(i - j, 0.0) + 0 * j
    P2 = np.maximum(j - i, 0.0) + 0 * i
    IF1 = (i + 1.0) + 0 * j
    IF2 = (128.0 - i) + 0 * j
    ZF = 127.0 - j
    ZB = j + 0.0
    n = np.arange(32, dtype=np.float32)[None, :]
    T1 = 4095.0 - 128.0 * n - j
    T2 = 128.0 * n + j
    sel = np.zeros((128, 64), np.float32)
    sel[64, :] = 1.0
    c_pat = np.concatenate([P1, P2, IF1, IF2, ZF, ZB, T1, T2, sel], axis=1).astype(np.float32)
    flags = np.zeros((128, 2), np.float32)
    flags[:, 0] = 1.0 if half == 1 else 0.0
    flags[:, 1] = 1.0 if half == 0 else 0.0
    return {"c_ident": np.eye(128, dtype=np.float32), "c_rp": c_rp, "c_rr": c_rr, "c_ax": c_ax,
            "c_pat": c_pat, "c_flags": flags}


_PARAMS = ["attn_norm", "w_in", "diff_q_norm", "diff_k_norm", "diff_lam_q1", "diff_lam_k1", "diff_lam_q2",
           "diff_lam_k2", "diff_subln", "ret_decay_fwd", "ret_decay_bwd", "ret_group_norm", "gqa_q_norm",
           "gqa_k_norm", "w_branch", "w_out", "mlp_norm", "w_mlp_in", "w_mlp_out"]


def make_in_maps(inputs):
    x = np.asarray(inputs["x"], dtype=np.float32)
    params = {k: np.ascontiguousarray(np.asarray(inputs[k], dtype=np.float32)) for k in _PARAMS}
    cst = [_consts(0), _consts(1)]
    in_maps = []
    for c in range(NCORES):
        b, half = c // 2, c % 2
        m = {"x": np.ascontiguousarray(x[b, half * T:(half + 1) * T, :])}
        m.update(params)
        m.update(cst[half])
        in_maps.append(m)
    return in_maps


def kernel(**inputs):
    nc = build(2)
    in_maps = make_in_maps(inputs)
    res = run_bass_kernel_spmd(nc, in_maps, core_ids=list(range(NCORES)))
    out = np.empty((4, SEQ, D), np.float32)
    for c in range(NCORES):
        b, half = c // 2, c % 2
        out[b, half * T:(half + 1) * T, :] = res.results[c]["out"]
    return out
```

```python
import math
import contextlib
import numpy as np
import concourse.bass as bass
import concourse.mybir as mybir
from concourse.bass_utils import run_bass_kernel_spmd

F32 = mybir.dt.float32
BF16 = mybir.dt.bfloat16
AF = mybir.ActivationFunctionType
ALU = mybir.AluOpType
AX = mybir.AxisListType

NCORES = 8
T = 4096
SEQ = 8192
D = 1024
NT = T // 128
INC = 7424
DFF = 4096
EPS = 1e-6
PAIRS = [[0, 1], [2, 3], [4, 5], [6, 7]]


class Op:
    __slots__ = ("eng", "fn", "deps", "dma", "milestone", "count", "sem", "cc")

    def __init__(self, eng, fn, dma):
        self.eng = eng
        self.fn = fn
        self.dma = dma
        self.cc = False
        self.deps = []
        self.milestone = False
        self.count = 0
        self.sem = None


class Sched:
    ENGS = ("pe", "act", "dve", "pool", "sp")

    def __init__(self, nc, n_dma_sems=16):
        self.nc = nc
        self.ops = {e: [] for e in self.ENGS}
        self.lastw = {}
        self.readers = {}
        self.n_dma_sems = n_dma_sems
        self.dmas = []

    def op(self, eng, fn, reads=(), writes=(), dma=False):
        o = Op(eng, fn, dma)
        deps = {}
        for t in reads:
            w = self.lastw.get(t)
            if w is not None:
                deps[id(w)] = (w, True)
        for t in writes:
            w = self.lastw.get(t)
            if w is not None and id(w) not in deps:
                deps[id(w)] = (w, w.dma or w.cc)
            for r in self.readers.get(t, ()):
                if id(r) not in deps:
                    deps[id(r)] = (r, r.dma or r.cc)
        for (d, raw) in deps.values():
            if d.eng == eng and not (d.dma or d.cc) and not raw:
                continue
            o.deps.append(d)
            if not d.dma:
                d.milestone = True
        for t in reads:
            self.readers.setdefault(t, []).append(o)
        for t in writes:
            self.lastw[t] = o
            self.readers[t] = []
        self.ops[eng].append(o)
        if dma:
            self.dmas.append(o)
        return o

    def barrier(self):
        lasts = []
        for e in self.ENGS:
            for o in reversed(self.ops[e]):
                if not o.dma:
                    lasts.append(o)
                    break
        for e in self.ENGS:
            j = Op(e, lambda eng: eng.nop(), False)
            for o in lasts:
                if o.eng != e:
                    j.deps.append(o)
                    o.milestone = True
            j.deps.extend(self.dmas)
            self.ops[e].append(j)
        self.dmas = []
        self.lastw = {}
        self.readers = {}

    def emit(self, stack):
        nc = self.nc
        esem = {e: stack.enter_context(nc.semaphore("s_" + e)) for e in self.ENGS}
        ccsem = stack.enter_context(nc.semaphore("s_cc"))
        ccn = 0
        dsem = {}
        for e in self.ENGS:
            if any(o.dma for o in self.ops[e]):
                dsem[e] = [stack.enter_context(nc.semaphore(f"d_{e}{i}")) for i in range(self.n_dma_sems)]
        for e in self.ENGS:
            c = 0
            k = 0
            uses = [0] * self.n_dma_sems
            for o in self.ops[e]:
                if o.cc:
                    ccn += 1
                    o.sem = ccsem
                    o.count = ccn
                    o.milestone = True
                elif o.dma:
                    j = k % self.n_dma_sems
                    uses[j] += 1
                    o.sem = dsem[e][j]
                    o.count = 16 * uses[j]
                    k += 1
                elif o.milestone:
                    c += 1
                    o.sem = esem[e]
                    o.count = c
        block = stack.enter_context(nc.Block())

        def run(ename, eng):
            waited = {}
            for o in self.ops[ename]:
                need = {}
                for d in o.deps:
                    key = id(d.sem)
                    if waited.get(key, 0) >= d.count:
                        continue
                    if key not in need or need[key][1] < d.count:
                        need[key] = (d.sem, d.count)
                if o.dma:
                    key = id(o.sem)
                    pv = o.count - 16
                    if pv > 0 and waited.get(key, 0) < pv:
                        if key not in need or need[key][1] < pv:
                            need[key] = (o.sem, pv)
                for key, (s, v) in need.items():
                    eng.wait_ge(s, v)
                    waited[key] = v
                ins = o.fn(eng)
                if o.dma:
                    ins.then_inc(o.sem, 16)
                elif o.milestone:
                    ins.then_inc(o.sem, 1)

        @block.tensor
        def _(e):
            run("pe", e)

        @block.scalar
        def _(e):
            run("act", e)

        @block.vector
        def _(e):
            run("dve", e)

        @block.gpsimd
        def _(e):
            run("pool", e)

        @block.sync
        def _(e):
            run("sp", e)


def build(n_layers=2, dbg=False, stop=99):
    nc = bass.Bass("TRN2", target_bir_lowering=False)
    S = Sched(nc)

    def din(name, shape, dt=F32):
        return nc.dram_tensor(name, list(shape), dt, kind="ExternalInput").ap()

    x_in = din("x", [T, D])
    p_attn_norm = din("attn_norm", [2, D])
    w_in = din("w_in", [2, D, INC])
    p_dqn = din("diff_q_norm", [2, 64])
    p_dkn = din("diff_k_norm", [2, 64])
    p_lq1 = din("diff_lam_q1", [2, 64])
    p_lk1 = din("diff_lam_k1", [2, 64])
    p_lq2 = din("diff_lam_q2", [2, 64])
    p_lk2 = din("diff_lam_k2", [2, 64])
    p_subln = din("diff_subln", [2, 128])
    p_rdf = din("ret_decay_fwd", [2, 4])
    p_rdb = din("ret_decay_bwd", [2, 4])
    p_rgn = din("ret_group_norm", [2, 512])
    p_gqn = din("gqa_q_norm", [2, 64])
    p_gkn = din("gqa_k_norm", [2, 64])
    w_branch = din("w_branch", [2, 3, 512, D])
    w_out = din("w_out", [2, D, D])
    p_mlp_norm = din("mlp_norm", [2, D])
    w_mlp_in = din("w_mlp_in", [2, D, DFF])
    w_mlp_out = din("w_mlp_out", [2, DFF, D])
    c_ident = din("c_ident", [128, 128])
    c_rp = din("c_rp", [T, 16])
    c_rr = din("c_rr", [T, 256])
    c_ax = din("c_ax", [T, 64])
    c_pat = din("c_pat", [128, 4 * 128 + 2 + 64 + 64])
    c_flags = din("c_flags", [128, 2])

    out_ext = nc.dram_tensor("out", [T, D], F32, kind="ExternalOutput").ap()

    def dscr(name, shape, dt=BF16):
        return nc.dram_tensor(name, list(shape), dt)

    QA = dscr("QA", [512, T]).ap()
    KAi = [dscr(f"KA{i}", [256, T]) for i in range(2)]; KAoi = [dscr(f"KAo{i}", [512, T]) for i in range(2)]
    KC = dscr("KC", [256, T]); KCo = dscr("KCo", [512, T])
    VAi = [dscr(f"VA{i}", [T // 2, 512]) for i in range(2)]; VAoi = [dscr(f"VAo{i}", [T, 512]) for i in range(2)]
    VC = dscr("VC", [T, 128]); VCo = dscr("VCo", [2 * T, 128])
    ST = dscr("ST", [1024, 128], F32); STo = dscr("STo", [2048, 128], F32)
    QC = dscr("QC", [512, T]).ap()
    RQT = dscr("RQT", [512, T]).ap()
    RKT = dscr("RKT", [512, T]).ap()
    RKtok = dscr("RKtok", [T, 512]).ap()
    RVtok = dscr("RVtok", [T, 512]).ap()
    RG = dscr("RG", [T, 512]).ap()
    GATES = dscr("GATES", [T, 3072]).ap()
    BR = dscr("BR", [3, 512, T]).ap()
    H1 = dscr("H1", [T, D], F32).ap()
    Hs = dscr("Hs", [T, D], F32).ap()
    U2T = dscr("U2T", [D, T]).ap()
    if dbg:
        dbg_out = {}
        for nm, shp, dt in (("d_BR", [3, 512, T], BF16), ("d_H1", [T, D], F32), ("d_STo", [2048, 128], F32)):
            dbg_out[nm] = nc.dram_tensor(nm, shp, dt, kind="ExternalOutput").ap()

    def phase_ctx(i):
        if stop >= i:
            with contextlib.ExitStack() as ph_:
                yield ph_

    uid = [0]

    def U(p):
        uid[0] += 1
        return f"{p}{uid[0]}"

    def DMA(out, in_, reads=(), writes=(), q="sp"):
        return S.op(q, lambda e: e.dma_start(out=out, in_=in_), reads, writes, dma=True)

    def DMA3(out, in_, reads=(), writes=(), step=8):
        n = out.shape[1]
        for a in range(0, n, step):
            DMA(out[:, a:a + step, :], in_[:, a:a + step, :], reads, [w + f"_{a}" for w in writes])
        return [w + f"_{a}" for w in writes for a in range(0, n, step)]

    def MM(out, lhsT, rhs, start, stop, reads, writes):
        return S.op("pe", lambda e: e.matmul(out, lhsT=lhsT, rhs=rhs, start=start, stop=stop), reads, writes)

    def TR(out, in_, ident, reads, writes):
        return S.op("pe", lambda e: e.transpose(out, in_, ident), reads, writes)

    def ACT(out, in_, func, reads, writes, scale=1.0, bias=0.0, accum=None):
        if accum is None:
            return S.op("act", lambda e: e.activation(out=out, in_=in_, func=func, bias=bias, scale=scale), reads, writes)
        return S.op("act", lambda e: e.activation(out=out, in_=in_, func=func, bias=bias, scale=scale, accum_out=accum), reads, writes)

    def ACOPY(out, in_, reads, writes):
        return S.op("act", lambda e: e.copy(out=out, in_=in_), reads, writes)

    def TT(eng, out, in0, in1, op, reads, writes):
        return S.op(eng, lambda e: e.tensor_tensor(out=out, in0=in0, in1=in1, op=op), reads, writes)

    def TS(eng, out, in0, s1, s2, op0, op1, reads, writes):
        assert eng == "dve"
        if s2 is None:
            return S.op(eng, lambda e: e.tensor_scalar(out=out, in0=in0, scalar1=s1, scalar2=None, op0=op0), reads, writes)
        return S.op(eng, lambda e: e.tensor_scalar(out=out, in0=in0, scalar1=s1, scalar2=s2, op0=op0, op1=op1), reads, writes)

    def STT(eng, out, in0, scalar, in1, op0, op1, reads, writes):
        return S.op(eng, lambda e: e.scalar_tensor_tensor(out=out, in0=in0, scalar=scalar, in1=in1, op0=op0, op1=op1), reads, writes)

    def CP(eng, out, in_, reads, writes):
        return S.op(eng, lambda e: e.tensor_copy(out=out, in_=in_), reads, writes)

    def RSUM(eng, out, in_, reads, writes):
        return S.op(eng, lambda e: e.reduce_sum(out=out, in_=in_, axis=AX.X), reads, writes)

    def MEMSET(eng, ap, val, writes):
        return S.op(eng, lambda e: e.memset(ap, val), (), writes)

    def RECIP(out, in_, reads, writes):
        return S.op("dve", lambda e: e.reciprocal(out=out, in_=in_), reads, writes)

    def CC(in_t, out_t, reads, writes):
        o = S.op("pool", lambda e: e.collective_compute("AllGather", ALU.bypass, replica_groups=PAIRS,
                                                        ins=[in_t.ap().opt()], outs=[out_t.ap().opt()]), reads, writes)
        o.cc = True
        return o

    with contextlib.ExitStack() as top:
        def sbT(st, name, shape, dt=F32):
            return st.enter_context(nc.sbuf_tensor(U(name), list(shape), dt))

        def psT(st, name, shape, dt=F32):
            return st.enter_context(nc.psum_tensor(U(name), list(shape), dt))

        ident_f = sbT(top, "identf", [128, 128])
        ident = sbT(top, "ident", [128, 128], BF16)
        ones_bf = sbT(top, "onesbf", [128, 128], BF16)
        ones_f = sbT(top, "onesf", [128, 128])
        pat = sbT(top, "pat", [128, 4 * 128 + 2 + 64 + 64])
        flags = sbT(top, "flags", [128, 2])
        rpt = sbT(top, "rpt", [128, NT, 16])
        axt = sbT(top, "axt", [128, NT, 64])
        DMA(ident_f[:], c_ident[:, :], (), ["identf"])
        CP("dve", ident[:], ident_f[:], ["identf"], ["ident"])
        MEMSET("dve", ones_bf[:], 1.0, ["onesbf"])
        MEMSET("dve", ones_f[:], 1.0, ["onesf"])
        epsb = sbT(top, "epsb", [128, 1])
        MEMSET("dve", epsb[:], 1e-5, ["epsb"])
        mhalf = sbT(top, "mhalf", [128, 512])
        MEMSET("dve", mhalf[:], -0.5, ["mhalf"])
        DMA(pat[:], c_pat[:, :], (), ["pat"])
        DMA(flags[:], c_flags[:, :], (), ["flags"])
        DMA3(rpt[:], c_rp.rearrange("(n p) c -> p n c", p=128), (), ["rpt"])
        DMA3(axt[:], c_ax.rearrange("(n p) c -> p n c", p=128), (), ["axt"])
        P1 = pat[:, 0:128]
        P2 = pat[:, 128:256]
        IF1 = pat[:, 256:384]
        IF2 = pat[:, 384:512]
        ZF = pat[:, 512:513]
        ZB = pat[:, 513:514]
        T1 = pat[:, 514:546]
        T2 = pat[:, 546:578]
        SEL = pat[:, 578:642]
        S.barrier()

        for l in range(n_layers):
            lam_init = 0.8 - 0.6 * math.exp(-0.3 * l)
            Hsrc = x_in if l == 0 else Hs
            Hdst = out_ext if l == n_layers - 1 else Hs

            with contextlib.ExitStack() as lay:
                g_attn = sbT(lay, "gattn", [128, D])
                g_mlp = sbT(lay, "gmlp", [128, D])
                g_dq = sbT(lay, "gdq", [128, 64])
                g_dk = sbT(lay, "gdk", [128, 64])
                g_gq = sbT(lay, "ggq", [128, 64])
                g_gk = sbT(lay, "ggk", [128, 64])
                lamv = sbT(lay, "lamv", [128, 4, 64])
                lamt = sbT(lay, "lamt", [128, 8])
                nlam = sbT(lay, "nlam", [128, 1])
                subc = sbT(lay, "subc", [128, 2])
                rdec = sbT(lay, "rdec", [128, 8])
                lfb = sbT(lay, "lfb", [128, 8])
                rgn = sbT(lay, "rgn", [128, 512])
                DMA(g_attn[:], p_attn_norm[l:l + 1, :].partition_broadcast(128), (), ["gattn"])
                DMA(g_mlp[:], p_mlp_norm[l:l + 1, :].partition_broadcast(128), (), ["gmlp"])
                DMA(g_dq[:], p_dqn[l:l + 1, :].partition_broadcast(128), (), ["gdq"])
                DMA(g_dk[:], p_dkn[l:l + 1, :].partition_broadcast(128), (), ["gdk"])
                DMA(g_gq[:], p_gqn[l:l + 1, :].partition_broadcast(128), (), ["ggq"])
                DMA(g_gk[:], p_gkn[l:l + 1, :].partition_broadcast(128), (), ["ggk"])
                for i, pv in enumerate((p_lq1, p_lk1, p_lq2, p_lk2)):
                    DMA(lamv[:, i, :], pv[l:l + 1, :].partition_broadcast(128), (), [f"lamv{i}"])
                DMA(subc[:, 0:1], p_subln[l:l + 1, :].rearrange("o p -> p o"), (), ["subc0"])
                DMA(rdec[:, 0:4], p_rdf[l:l + 1, :].partition_broadcast(128), (), ["rdec0"])
                DMA(rdec[:, 4:8], p_rdb[l:l + 1, :].partition_broadcast(128), (), ["rdec1"])
                DMA(rgn[:], p_rgn[l:l + 1, :].partition_broadcast(128), (), ["rgn"])
                TT("dve", lamv[:, 0, :], lamv[:, 0, :], lamv[:, 1, :], ALU.mult, ["lamv0", "lamv1"], ["lamp0"])
                TT("dve", lamv[:, 2, :], lamv[:, 2, :], lamv[:, 3, :], ALU.mult, ["lamv2", "lamv3"], ["lamp1"])
                RSUM("dve", lamt[:, 0:1], lamv[:, 0, :], ["lamp0"], ["lams0"])
                RSUM("dve", lamt[:, 1:2], lamv[:, 2, :], ["lamp1"], ["lams1"])
                ACT(lamt[:, 2:4], lamt[:, 0:2], AF.Exp, ["lams0", "lams1"], ["lame"])
                STT("dve", nlam[:, 0:1], lamt[:, 3:4], -lam_init, lamt[:, 2:3], ALU.add, ALU.subtract, ["lame"], ["nlam"])
                TS("dve", subc[:, 1:2], subc[:, 0:1], 1.0 - lam_init, None, ALU.mult, None, ["subc0"], ["subc1"])
                ACT(lfb[:], rdec[:], AF.Exp, ["rdec0", "rdec1"], ["lfbe"])
                TS("dve", lfb[:], lfb[:], -1.0, None, ALU.mult, None, ["lfbe"], ["lfb"])
                S.barrier()

                for ph in phase_ctx(1):
                    uT = sbT(ph, "uT", [128, 8, T], BF16)
                    ht = [sbT(ph, "ht", [128, D]) for _ in range(2)]
                    junk = sbT(ph, "junk", [128, D], BF16)
                    ss1 = [sbT(ph, "ss1", [128, 2]) for _ in range(2)]
                    ub = [sbT(ph, "ub", [128, D], BF16) for _ in range(2)]
                    ptr = [psT(ph, "ptr", [128, 8, 128], BF16) for _ in range(2)]
                    for ti in range(NT):
                        k = ti % 2
                        DMA(ht[k][:], Hsrc[ti * 128:(ti + 1) * 128, :], (), [f"ht{k}"])
                        ACT(junk[:], ht[k][:], AF.Square, [f"ht{k}"], ["junk", f"ssa{k}"], accum=ss1[k][:, 0:1])
                        TS("dve", ss1[k][:, 1:2], ss1[k][:, 0:1], 1.0 / D, EPS, ALU.mult, ALU.add, [f"ssa{k}"], [f"ssb{k}"])
                        TT("pool", ss1[k][:, 1:2], ss1[k][:, 1:2], mhalf[:, 0:1], ALU.pow, [f"ssb{k}"], [f"ssc{k}"])
                        STT("dve", ub[k][:], ht[k][:], ss1[k][:, 1:2], g_attn[:], ALU.mult, ALU.mult,
                            [f"ht{k}", f"ssc{k}"], [f"ub{k}"])
                        for c in range(8):
                            TR(ptr[k][:, c, :], ub[k][:, c * 128:(c + 1) * 128], ident[:], [f"ub{k}"], [f"ptr{k}"])
                        ACOPY(uT[:, :, ti * 128:(ti + 1) * 128], ptr[k][:], [f"ptr{k}"], ["uT"])

                    wst = [sbT(ph, "wst", [128, 8, 512]) for _ in range(2)]
                    wb = [sbT(ph, "wb", [128, 8, 512], BF16) for _ in range(2)]
                    pp = [psT(ph, "pp", [128, 512]) for _ in range(3)]
                    pt4 = [psT(ph, "pt4", [128, 4, 128], BF16) for _ in range(2)]
                    col = [sbT(ph, "col", [128, 4, 512], BF16) for _ in range(2)]
                    sq = sbT(ph, "sq", [128, 512])
                    ssgs = [sbT(ph, "ssg", [128, 16]) for _ in range(2)]
                    xn = sbT(ph, "xn", [128, 512])
                    xg = [sbT(ph, "xg", [128, 512]) for _ in range(2)]
                    xb = [sbT(ph, "xb", [128, 512], BF16) for _ in range(2)]
                    xk = sbT(ph, "xk", [128, 128], BF16)
                    tmp = [sbT(ph, "tmp", [128, 256]) for _ in range(4)]
                    rrt = [sbT(ph, "rrt", [128, 128]) for _ in range(2)]
                    vb = [sbT(ph, "vb", [128, 512], BF16) for _ in range(2)]
                    sg = sbT(ph, "sg", [128, 512])

                    def load_w(cg):
                        k = cg % 2
                        off, ncols = CGS[cg][1], CGS[cg][2]
                        DMA(wst[k][:, :, 0:ncols], w_in[l, :, off:off + ncols].rearrange("(c p) n -> p c n", p=128),
                            (), [f"wst{k}"])
                        CP("pool", wb[k][:, :, 0:ncols], wst[k][:, :, 0:ncols], [f"wst{k}"], [f"wb{k}"])

                    CGS = [("dq", 0, 512), ("dk", 512, 512), ("dv", 1024, 512), ("rq", 1536, 512), ("rk", 2048, 512),
                           ("rv", 2560, 512), ("rg", 3072, 512), ("gq", 3584, 512), ("gkv", 4096, 256)]
                    CGS += [("gate", 4352 + 512 * i, 512) for i in range(6)]

                    def rope(xgv1, xgv2, cv, sv, o1, o2, shape, rtags, otag, after=()):
                        n = 1
                        for s_ in shape[1:]:
                            n *= s_
                        pat_ = {2: "p (a b) -> p a b", 3: "p (a b c) -> p a b c"}[len(shape) - 1]
                        kw = {"a": shape[1], "b": shape[2]}
                        if len(shape) == 4:
                            kw["c"] = shape[3]
                        tv = [t_[:, 0:n].rearrange(pat_, **kw) for t_ in tmp]
                        TT("dve", tv[0], xgv1, cv, ALU.mult, rtags, ["tmp0"])
                        TT("dve", tv[1], xgv2, sv, ALU.mult, rtags, ["tmp1"])
                        TT("dve", o1, tv[0], tv[1], ALU.subtract, ["tmp0", "tmp1"] + list(after), [otag + "a"])
                        TT("pool", tv[2], xgv2, cv, ALU.mult, rtags, ["tmp2"])
                        TT("pool", tv[3], xgv1, sv, ALU.mult, rtags, ["tmp3"])
                        TT("pool", o2, tv[2], tv[3], ALU.add, ["tmp2", "tmp3"] + list(after), [otag + "b"])

                    def rmsA(ppv, ncols, G, gd, ppt, sl):
                        ssg = ssgs[sl]
                        ACT(sq[:, 0:ncols], ppv, AF.Square, [ppt], ["sq"])
                        RSUM("dve", ssg[:, 0:G], sq[:, 0:ncols].rearrange("p (g d) -> p g d", d=gd), ["sq"], [f"ssg{sl}"])
                        TS("dve", ssg[:, 0:G], ssg[:, 0:G], 1.0 / gd, EPS, ALU.mult, ALU.add, [f"ssg{sl}"], [f"ssg{sl}"])
                        TT("pool", ssg[:, 0:G], ssg[:, 0:G], mhalf[:, 0:G], ALU.pow, [f"ssg{sl}"], [f"ssg{sl}"])

                    def rms(ppv, ncols, G, gd, gain, ppt, xgt, sl):
                        ssg = ssgs[sl]
                        TT("dve", xn[:, 0:ncols].rearrange("p (g d) -> p g d", d=gd),
                           ppv.rearrange("p (g d) -> p g d", d=gd),
                           ssg[:, 0:G].unsqueeze(2).broadcast_to([128, G, gd]), ALU.mult, [ppt, f"ssg{sl}"], ["xn"])
                        TT("dve", xgt[0][:, 0:ncols].rearrange("p (g d) -> p g d", d=gd),
                           xn[:, 0:ncols].rearrange("p (g d) -> p g d", d=gd),
                           gain[:, :].unsqueeze(1).broadcast_to([128, G, gd]), ALU.mult, ["xn"], [xgt[1]])

                    def postA(kind, k3, sl):
                        ppk = pp[k3]
                        ppt = f"pp{k3}"
                        if kind in ("dq", "dk", "gq"):
                            rmsA(ppk[:, 0:512], 512, 8, 64, ppt, sl)
                        elif kind == "gkv":
                            rmsA(ppk[:, 0:128], 128, 2, 64, ppt, sl)

                    def post(cg, kind, ti, k, k3, sl):
                        ppk = pp[k3]
                        ppt = f"pp{k3}"
                        rows = slice(ti * 128, (ti + 1) * 128)
                        xgk, xbk = xg[k], xb[k]
                        xgt, xbt = f"xg{k}", f"xb{k}"
                        nblk = 0
                        if kind in ("dq", "dk"):
                            gain = g_dq if kind == "dq" else g_dk
                            rms(ppk[:, 0:512], 512, 8, 64, gain, ppt, (xgk, xgt), sl)
                            ACOPY(xbk[:], xgk[:], [xgt], [xbt])
                            xv = xgk[:, :].rearrange("p (g d) -> p g d", d=64)
                            ov = xbk[:, :].rearrange("p (g d) -> p g d", d=64)
                            cv = rpt[:, ti, 0:8].unsqueeze(1).broadcast_to([128, 8, 8])
                            sv = rpt[:, ti, 8:16].unsqueeze(1).broadcast_to([128, 8, 8])
                            rope(xv[:, :, 0:8], xv[:, :, 8:16], cv, sv, ov[:, :, 0:8], ov[:, :, 8:16],
                                 [128, 8, 8], [xgt, "rpt"], xbt, after=[xbt])
                            nblk = 4
                            dst = QA if kind == "dq" else None
                        elif kind in ("rq", "rk"):
                            r0 = 0 if kind == "rq" else 128
                            DMA(rrt[k][:], c_rr[rows, r0:r0 + 128], (), [f"rrt{k}"])
                            ACOPY(xgk[:], ppk[:], [ppt], [xgt])
                            xv = xgk[:, :].rearrange("p (g d) -> p g d", d=128)
                            ov = xbk[:, :].rearrange("p (g d) -> p g d", d=128)
                            cv = rrt[k][:, 0:64].unsqueeze(1).broadcast_to([128, 4, 64])
                            sv = rrt[k][:, 64:128].unsqueeze(1).broadcast_to([128, 4, 64])
                            rope(xv[:, :, 0:64], xv[:, :, 64:128], cv, sv, ov[:, :, 0:64], ov[:, :, 64:128],
                                 [128, 4, 64], [xgt, f"rrt{k}"], xbt)
                            nblk = 4
                            dst = RQT if kind == "rq" else RKT
                            if kind == "rk":
                                DMA(RKtok[rows, :], xbk[:], [xbt + "a", xbt + "b"], [])
                        elif kind == "gq":
                            rms(ppk[:, 0:512], 512, 8, 64, g_gq, ppt, (xgk, xgt), sl)
                            xv = xgk[:, :].rearrange("p (g a f d) -> p g a f d", a=2, f=2, d=16)
                            ov = xbk[:, :].rearrange("p (g a f d) -> p g a f d", a=2, f=2, d=16)
                            av = axt[:, ti, :].rearrange("p (cs a d) -> p cs a d", cs=2, a=2)
                            cv = av[:, 0, :, :].unsqueeze(1).broadcast_to([128, 8, 2, 16])
                            sv = av[:, 1, :, :].unsqueeze(1).broadcast_to([128, 8, 2, 16])
                            rope(xv[:, :, :, 0, :], xv[:, :, :, 1, :], cv, sv, ov[:, :, :, 0, :], ov[:, :, :, 1, :],
                                 [128, 8, 2, 16], [xgt, "axt"], xbt)
                            nblk = 4
                            dst = QC
                        elif kind == "gkv":
                            rms(ppk[:, 0:128], 128, 2, 64, g_gk, ppt, (xgk, xgt), sl)
                            xv = xgk[:, 0:128].rearrange("p (g a f d) -> p g a f d", a=2, f=2, d=16)
                            ov = xk[:, :].rearrange("p (g a f d) -> p g a f d", a=2, f=2, d=16)
                            av = axt[:, ti, :].rearrange("p (cs a d) -> p cs a d", cs=2, a=2)
                            cv = av[:, 0, :, :].unsqueeze(1).broadcast_to([128, 2, 2, 16])
                            sv = av[:, 1, :, :].unsqueeze(1).broadcast_to([128, 2, 2, 16])
                            rope(xv[:, :, :, 0, :], xv[:, :, :, 1, :], cv, sv, ov[:, :, :, 0, :], ov[:, :, :, 1, :],
                                 [128, 2, 2, 16], [xgt, "axt"], "xk")
                            CP("dve", xbk[:, 0:256].rearrange("p (g u d) -> p g u d", u=2, d=64),
                               xk[:, :].rearrange("p (g d) -> p g d", d=64).unsqueeze(2).broadcast_to([128, 2, 2, 64]),
                               ["xka", "xkb"], [xbt])
                            ACOPY(vb[k][:, 0:128], ppk[:, 128:256], [ppt], [f"vb{k}"])
                            DMA(VC.ap()[rows, :], vb[k][:, 0:128], [f"vb{k}"], [])
                            nblk = 2
                            dst = KC.ap()
                        elif kind in ("dv", "rv"):
                            ACOPY(vb[k][:], ppk[:], [ppt], [f"vb{k}"])
                            if kind == "dv":
                                DMA(VAi[ti // 16].ap()[(ti % 16) * 128:(ti % 16 + 1) * 128, :], vb[k][:], [f"vb{k}"], [])
                            else:
                                DMA(RVtok[rows, :], vb[k][:], [f"vb{k}"], [])
                        elif kind == "rg":
                            ACT(sg[:], ppk[:], AF.Sigmoid, [ppt], ["sg"])
                            TT("dve", vb[k][:], sg[:], ppk[:], ALU.mult, ["sg", ppt], [f"vb{k}"])
                            DMA(RG[rows, :], vb[k][:], [f"vb{k}"], [])
                        else:
                            gi = cg - 9
                            ACT(vb[k][:], ppk[:], AF.Sigmoid, [ppt], [f"vb{k}"])
                            DMA(GATES[rows, gi * 512:(gi + 1) * 512], vb[k][:], [f"vb{k}"], [])
                        if nblk:
                            ck = (ti // 4) % 2
                            for b_ in range(nblk):
                                TR(pt4[k][:, b_, :], xbk[:, b_ * 128:(b_ + 1) * 128], ident[:], [xbt, xbt + "a", xbt + "b"], [f"pt4{k}"])
                            ACOPY(col[ck][:, 0:nblk, (ti % 4) * 128:(ti % 4 + 1) * 128], pt4[k][:, 0:nblk, :],
                                  [f"pt4{k}"], [f"col{ck}"])
                            if ti % 4 == 3:
                                tb = ti // 4
                                if dst is None:
                                    for i2 in range(2):
                                        DMA(KAi[i2].ap()[:, tb * 512:(tb + 1) * 512].rearrange("(b p) t -> p b t", p=128),
                                            col[ck][:, 2 * i2:2 * i2 + 2, :], [f"col{ck}"], [])
                                else:
                                    DMA(dst[0:nblk * 128, tb * 512:(tb + 1) * 512].rearrange("(b p) t -> p b t", p=128),
                                        col[ck][:, 0:nblk, :], [f"col{ck}"], [])

                    load_w(0)
                    it = 0
                    hist = []
                    for cg in range(15):
                        kind, off, ncols = CGS[cg]
                        if cg + 1 < 15:
                            load_w(cg + 1)
                        wk = cg % 2
                        for ti in range(NT):
                            k = it % 2
                            k3 = it % 3
                            it += 1
                            ppk = pp[k3]
                            ppt = f"pp{k3}"
                            for c in range(8):
                                MM(ppk[:, 0:ncols], uT[:, c, ti * 128:(ti + 1) * 128], wb[wk][:, c, 0:ncols],
                                   c == 0, c == 7, ["uT", f"wb{wk}"], [ppt])
                            hist.append((cg, kind, ti, k, k3, k))
                            if len(hist) >= 2:
                                postA(hist[-2][1], hist[-2][4], hist[-2][5])
                            if len(hist) >= 3:
                                post(*hist[-3])
                    postA(hist[-1][1], hist[-1][4], hist[-1][5])
                    post(*hist[-2])
                    post(*hist[-1])
                    S.barrier()

                for ph in phase_ctx(2):
                    kt = [sbT(ph, "kt", [128, NT, 128], BF16) for _ in range(2)]
                    vt = [sbT(ph, "vt", [128, NT, 128], BF16) for _ in range(2)]
                    vf = sbT(ph, "vf", [128, NT, 128], BF16)
                    vbk = sbT(ph, "vbk", [128, NT, 128], BF16)
                    wfb = sbT(ph, "wfb", [128, 2, NT])
                    psf = psT(ph, "psf", [128, 128])
                    psb = psT(ph, "psb", [128, 128])
                    sst = [sbT(ph, "sst", [128, 2, 128]) for _ in range(2)]
                    for h in range(4):
                        k = h % 2
                        ktg = DMA3(kt[k][:], RKtok[:, h * 128:(h + 1) * 128].rearrange("(n p) d -> p n d", p=128), (), [f"kt{k}"])
                        vtg = DMA3(vt[k][:], RVtok[:, h * 128:(h + 1) * 128].rearrange("(n p) d -> p n d", p=128), (), [f"vt{k}"])
                        ACT(wfb[:, 0, :], T1, AF.Exp, [], ["wf"], scale=lfb[:, h:h + 1])
                        ACT(wfb[:, 1, :], T2, AF.Exp, [], ["wb_"], scale=lfb[:, 4 + h:5 + h])
                        TT("dve", vf[:], vt[k][:], wfb[:, 0, :].unsqueeze(2).broadcast_to([128, NT, 128]), ALU.mult,
                           vtg + ["wf"], ["vf"])
                        TT("pool", vbk[:], vt[k][:], wfb[:, 1, :].unsqueeze(2).broadcast_to([128, NT, 128]), ALU.mult,
                           vtg + ["wb_"], ["vbk"])
                        for n in range(NT):
                            MM(psf[:], kt[k][:, n, :], vf[:, n, :], n == 0, n == NT - 1, ktg + ["vf"], ["psf"])
                        for n in range(NT):
                            MM(psb[:], kt[k][:, n, :], vbk[:, n, :], n == 0, n == NT - 1, ktg + ["vbk"], ["psb"])
                        ACOPY(sst[k][:, 0, :], psf[:], ["psf"], [f"sstf{k}"])
                        CP("dve", sst[k][:, 1, :], psb[:], ["psb"], [f"sstb{k}"])
                        DMA(ST.ap()[h * 128:(h + 1) * 128, :], sst[k][:, 0, :], [f"sstf{k}"], [])
                        DMA(ST.ap()[512 + h * 128:512 + (h + 1) * 128, :], sst[k][:, 1, :], [f"sstb{k}"], [])
                    S.barrier()
                    CC(KAi[0], KAoi[0], (), ["cc"])
                    CC(KAi[1], KAoi[1], ["cc"], ["cc"])
                    CC(KC, KCo, ["cc"], ["cc"])
                    CC(VAi[0], VAoi[0], ["cc"], ["cc"])
                    CC(VAi[1], VAoi[1], ["cc"], ["cc"])
                    CC(VC, VCo, ["cc"], ["cc"])
                    CC(ST, STo, ["cc"], ["cc"])
                    S.op("pool", lambda e: e.nop(), ["cc"], ["cc2"])
                    S.barrier()
                if dbg and l == 0 and stop >= 2:
                    DMA(dbg_out["d_STo"], STo.ap(), (), [])
                    S.barrier()

                for ph in phase_ctx(3):
                    kT = [sbT(ph, "kT", [128, SEQ], BF16) for _ in range(2)]
                    vv = [sbT(ph, "vv", [128, 64, 128], BF16) for _ in range(2)]
                    qT = [sbT(ph, "qT", [128, T], BF16) for _ in range(2)]
                    Eb = [sbT(ph, "Eb", [128, 1024], BF16) for _ in range(3)]
                    Es = [sbT(ph, "Es", [128, 1024], BF16) for _ in range(2)]
                    sc = [psT(ph, "sc", [128, 1024]) for _ in range(2)]
                    po = [psT(ph, "po", [128, 512]) for _ in range(2)]
                    pS = [psT(ph, "pS", [128, 512]) for _ in range(2)]
                    rs = [sbT(ph, "rs", [128, 512]) for _ in range(2)]
                    on = [sbT(ph, "on", [128, 512]) for _ in range(2)]
                    dd = sbT(ph, "dd", [128, 512])
                    dsq = sbT(ph, "dsq", [128, 512])
                    rstd = sbT(ph, "rstd", [128, 512])
                    ao = [sbT(ph, "ao", [128, 512], BF16) for _ in range(2)]

                    def loadA(h):
                        k = h % 2
                        for r in range(2):
                            DMA(kT[k][:, r * T:(r + 1) * T],
                                KAoi[h // 2].ap()[r * 256 + (h % 2) * 128:r * 256 + (h % 2 + 1) * 128, :], (), [f"kT{k}{r}"])
                            for i2 in range(2):
                                DMA3(vv[k][:, r * 32 + i2 * 16:r * 32 + (i2 + 1) * 16, :],
                                     VAoi[i2].ap()[r * 2048:(r + 1) * 2048, h * 128:(h + 1) * 128].rearrange("(n p) d -> p n d", p=128),
                                     (), [f"vv{k}{r}{i2}"])
                        DMA(qT[k][:], QA[h * 128:(h + 1) * 128, :], (), [f"qT{k}"])

                    loadA(0)
                    ecnt = 0
                    scnt = 0
                    for h in range(4):
                        k = h % 2
                        if h + 1 < 4:
                            loadA(h + 1)
                        ktags = [f"kT{k}0", f"kT{k}1"]
                        vtags = [[f"vv{k}{r}{i2}_{a}" for i2 in range(2) for a in range(0, 16, 8)] for r in range(2)]
                        for qb in range(8):
                            qs = slice(qb * 512, (qb + 1) * 512)

                            def QK(kc):
                                for m in range(2):
                                    MM(sc[kc % 2][:, m * 512:(m + 1) * 512], kT[k][64 * m:64 * m + 64, kc * 128:(kc + 1) * 128],
                                       qT[k][64 * m:64 * m + 64, qs], True, True, [ktags[kc // 32], f"qT{k}"], [f"sc{kc % 2}"])

                            def SUMMM(sb_, first, last):
                                for m in range(2):
                                    MM(pS[m][:], ones_bf[:], Es[sb_][:, m * 512:(m + 1) * 512], first, last,
                                       [f"Es{sb_}"], [f"pS{m}"])

                            pend_sum = None
                            QK(0)
                            QK(1)
                            for kc in range(64):
                                e3 = kc % 3
                                ACT(Eb[e3][:], sc[kc % 2][:], AF.Exp, [f"sc{kc % 2}"], [f"E{e3}"], scale=0.125)
                                if kc + 2 < 64:
                                    QK(kc + 2)
                                for m in range(2):
                                    MM(po[m][:], vv[k][:, kc, :], Eb[e3][:, m * 512:(m + 1) * 512], kc == 0, kc == 63,
                                       vtags[kc // 32] + [f"E{e3}"], [f"po{m}"])
                                if kc % 2 == 1:
                                    if pend_sum is not None:
                                        SUMMM(*pend_sum)
                                    sb_ = scnt % 2
                                    scnt += 1
                                    TT("dve", Es[sb_][:], Eb[(kc - 1) % 3][:], Eb[e3][:], ALU.add,
                                       [f"E{(kc - 1) % 3}", f"E{e3}"], [f"Es{sb_}"])
                                    pend_sum = (sb_, kc == 1, kc == 63)
                            SUMMM(*pend_sum)
                            for m in range(2):
                                ACOPY(on[m][:], po[m][:], [f"po{m}"], [f"on{m}"])
                            for m in range(2):
                                RECIP(rs[m][:], pS[m][:], [f"pS{m}"], [f"rs{m}"])
                                TT("dve", on[m][:], on[m][:], rs[m][:], ALU.mult, [f"on{m}", f"rs{m}"], [f"on{m}"])
                            STT("dve", dd[:], on[1][:], nlam[:, 0:1], on[0][:], ALU.mult, ALU.add, ["on0", "on1"], ["dd"])
                            TT("pool", dsq[:], dd[:], dd[:], ALU.mult, ["dd"], ["dsq"])
                            MM(pS[0][:], ones_f[:], dsq[:], True, True, ["dsq"], ["pS0"])
                            ACT(rstd[:], pS[0][:], AF.Ln, ["pS0"], ["rstd"], scale=1.0 / 128, bias=epsb[:, 0:1])
                            ACT(rstd[:], rstd[:], AF.Exp, ["rstd"], ["rstd2"], scale=-0.5)
                            a_ = ao[ecnt % 2]
                            at = f"ao{ecnt % 2}"
                            ecnt += 1
                            STT("dve", a_[:], dd[:], subc[:, 1:2], rstd[:], ALU.mult, ALU.mult, ["dd", "rstd2"], [at])
                            DMA(BR[0, h * 128:(h + 1) * 128, qs], a_[:], [at], [])
                    S.barrier()

                for ph in phase_ctx(4):
                    kT = [sbT(ph, "kTc", [128, SEQ], BF16) for _ in range(2)]
                    vv = [sbT(ph, "vvc", [128, 64, 128], BF16) for _ in range(2)]
                    qT = [sbT(ph, "qTc", [128, T], BF16) for _ in range(2)]
                    Eb = [sbT(ph, "Ebc", [128, 1024], BF16) for _ in range(3)]
                    sc = [psT(ph, "scc", [128, 1024]) for _ in range(2)]
                    po = [psT(ph, "poc", [128, 512]) for _ in range(2)]
                    pb = psT(ph, "pbc", [128, 512])
                    oc = [sbT(ph, "oc", [128, 512]) for _ in range(2)]
                    rs = [sbT(ph, "rsc", [128, 512]) for _ in range(2)]
                    co = [sbT(ph, "co", [128, 512], BF16) for _ in range(2)]
                    for k in range(2):
                        MEMSET("dve", vv[k][:, :, 64:128], 1.0, [f"vone{k}"])

                    def loadC(p):
                        k = p % 2
                        g = p // 2
                        for r in range(2):
                            DMA(kT[k][:, r * T:(r + 1) * T], KCo.ap()[r * 256 + g * 128:r * 256 + (g + 1) * 128, :], (), [f"kT{k}{r}"])
                            DMA3(vv[k][:, r * 32:(r + 1) * 32, 0:64],
                                 VCo.ap()[r * T:(r + 1) * T, g * 64:(g + 1) * 64].rearrange("(n p) d -> p n d", p=128),
                                 (), [f"vv{k}{r}"])
                        DMA(qT[k][:], QC[p * 128:(p + 1) * 128, :], (), [f"qT{k}"])

                    loadC(0)
                    ecnt = 0
                    for p in range(4):
                        k = p % 2
                        if p + 1 < 4:
                            loadC(p + 1)
                        ktags = [f"kT{k}0", f"kT{k}1"]
                        vtags = [[f"vv{k}{r}_{a}" for a in range(0, 32, 8)] for r in range(2)]
                        for qb in range(8):
                            qs = slice(qb * 512, (qb + 1) * 512)

                            def QKc(kc):
                                for j in range(2):
                                    MM(sc[kc % 2][:, j * 512:(j + 1) * 512], kT[k][64 * j:64 * j + 64, kc * 128:(kc + 1) * 128],
                                       qT[k][64 * j:64 * j + 64, qs], True, True, [ktags[kc // 32], f"qT{k}"], [f"sc{kc % 2}"])

                            QKc(0)
                            QKc(1)
                            for kc in range(64):
                                e3 = kc % 3
                                ACT(Eb[e3][:], sc[kc % 2][:], AF.Exp, [f"sc{kc % 2}"], [f"E{e3}"], scale=0.125)
                                if kc + 2 < 64:
                                    QKc(kc + 2)
                                for j in range(2):
                                    MM(po[j][:], vv[k][:, kc, :], Eb[e3][:, j * 512:(j + 1) * 512], kc == 0, kc == 63,
                                       vtags[kc // 32] + [f"vone{k}", f"E{e3}"], [f"po{j}"])
                            for j in range(2):
                                CP("dve", oc[j][:], po[j][:], [f"po{j}"], [f"oc{j}"])
                            for j in range(2):
                                MM(pb[0:64, :], SEL[:, :], oc[j][:], True, True, [f"oc{j}"], ["pb"])
                                RECIP(rs[j][0:64, :], pb[0:64, :], ["pb"], [f"rs{j}"])
                                c_ = co[ecnt % 2]
                                ct = f"co{ecnt % 2}"
                                ecnt += 1
                                TT("dve", c_[0:64, :], oc[j][0:64, :], rs[j][0:64, :], ALU.mult, [f"oc{j}", f"rs{j}"], [ct])
                                DMA(BR[2, p * 128 + j * 64:p * 128 + (j + 1) * 64, qs], c_[0:64, :], [ct], [])
                    S.barrier()

                for ph in phase_ctx(5):
                    qTr = [sbT(ph, "qTr", [128, T], BF16) for _ in range(2)]
                    kTr = [sbT(ph, "kTr", [128, T], BF16) for _ in range(2)]
                    kt = [sbT(ph, "ktr", [128, NT, 128], BF16) for _ in range(2)]
                    vt = [sbT(ph, "vtr", [128, NT, 128], BF16) for _ in range(2)]
                    gt = [sbT(ph, "gtr", [128, NT, 128], BF16) for _ in range(2)]
                    vzf = sbT(ph, "vzf", [128, NT, 128], BF16)
                    vzb = sbT(ph, "vzb", [128, NT, 128], BF16)
                    qxf = sbT(ph, "qxf", [128, NT, 128], BF16)
                    qxb = sbT(ph, "qxb", [128, NT, 128], BF16)
                    SB = sbT(ph, "SBst", [128, NT, 128], BF16)
                    Dm = sbT(ph, "Dm", [128, 128])
                    dt1 = sbT(ph, "dt1", [128, 128])
                    Xf = sbT(ph, "Xf", [128, 128])
                    Xb = sbT(ph, "Xb", [128, 128])
                    zc = sbT(ph, "zc", [128, 4])
                    Fst = sbT(ph, "Fst", [128, 128])
                    Bst = sbT(ph, "Bst", [128, 128])
                    Fbf = [sbT(ph, "Fbf", [128, 128], BF16) for _ in range(2)]
                    sti = sbT(ph, "sti", [128, 2, 128])
                    scD = [sbT(ph, "scD", [128, 128], BF16) for _ in range(2)]
                    gs = sbT(ph, "gs", [128, 8])
                    xc = sbT(ph, "xc", [128, 128])
                    jk = sbT(ph, "jk", [128, 128])
                    yt = sbT(ph, "yt", [128, 128])
                    rb = [sbT(ph, "rb", [128, 128], BF16) for _ in range(2)]
                    roT = [sbT(ph, "roT", [128, 512], BF16) for _ in range(2)]
                    pkv = [psT(ph, "pkv", [128, 128]) for _ in range(2)]
                    psc = [psT(ph, "psc", [128, 128]) for _ in range(2)]
                    pov = [psT(ph, "pov", [128, 128]) for _ in range(2)]
                    ptt = [psT(ph, "ptt", [128, 128], BF16) for _ in range(2)]

                    def loadR(h):
                        k = h % 2
                        DMA(qTr[k][:], RQT[h * 128:(h + 1) * 128, :], (), [f"qTr{k}"])
                        DMA(kTr[k][:], RKT[h * 128:(h + 1) * 128, :], (), [f"kTr{k}"])
                        DMA3(kt[k][:], RKtok[:, h * 128:(h + 1) * 128].rearrange("(n p) d -> p n d", p=128), (), [f"kt{k}"])
                        DMA3(vt[k][:], RVtok[:, h * 128:(h + 1) * 128].rearrange("(n p) d -> p n d", p=128), (), [f"vt{k}"])
                        DMA3(gt[k][:], RG[:, h * 128:(h + 1) * 128].rearrange("(n p) d -> p n d", p=128), (), [f"gt{k}"])

                    loadR(0)
                    cnt = 0
                    for h in range(4):
                        k = h % 2
                        if h + 1 < 4:
                            loadR(h + 1)
                        lf = lfb[:, h:h + 1]
                        lb = lfb[:, 4 + h:5 + h]
                        ktg = [f"kt{k}_{a_}" for a_ in range(0, 32, 8)]
                        vtg = [f"vt{k}_{a_}" for a_ in range(0, 32, 8)]
                        gtg = [f"gt{k}_{a_}" for a_ in range(0, 32, 8)]
                        TS("dve", dt1[:], P1, lf, None, ALU.mult, None, [], ["dt1"])
                        STT("dve", dt1[:], P2, lb, dt1[:], ALU.mult, ALU.add, ["dt1"], ["dt1"])
                        ACT(Dm[:], dt1[:], AF.Exp, ["dt1"], ["Dm"])
                        ACT(Xf[:], IF1, AF.Exp, [], ["Xf"], scale=lf)
                        ACT(Xb[:], IF2, AF.Exp, [], ["Xb"], scale=lb)
                        ACT(zc[:, 0:1], ZF, AF.Exp, [], ["zc0"], scale=lf)
                        ACT(zc[:, 1:2], ZB, AF.Exp, [], ["zc1"], scale=lb)
                        ACT(zc[:, 2:3], lf, AF.Exp, [], ["zc2"], scale=128.0)
                        ACT(zc[:, 3:4], lb, AF.Exp, [], ["zc3"], scale=128.0)
                        TS("dve", vzf[:], vt[k][:], zc[:, 0:1], None, ALU.mult, None, vtg + ["zc0"], ["vzf"])
                        TT("pool", vzb[:], vt[k][:], zc[:, 1:2].unsqueeze(2).broadcast_to([128, NT, 128]), ALU.mult, vtg + ["zc1"], ["vzb"])
                        TT("dve", qxf[:], qTr[k][:, :].rearrange("p (n i) -> p n i", i=128),
                           Xf[:, :].unsqueeze(1).broadcast_to([128, NT, 128]), ALU.mult, [f"qTr{k}", "Xf"], ["qxf"])
                        TT("pool", qxb[:], qTr[k][:, :].rearrange("p (n i) -> p n i", i=128),
                           Xb[:, :].unsqueeze(1).broadcast_to([128, NT, 128]), ALU.mult, [f"qTr{k}", "Xb"], ["qxb"])
                        DMA(sti[:, 0, :], STo.ap()[h * 128:(h + 1) * 128, :], (), ["sti0"])
                        DMA(sti[:, 1, :], STo.ap()[1024 + 512 + h * 128:1024 + 512 + (h + 1) * 128, :], (), ["sti1"])
                        TS("dve", Fst[:], sti[:, 0, :], flags[:, 0:1], None, ALU.mult, None, ["sti0"], ["Fst"])
                        TS("dve", Bst[:], sti[:, 1, :], flags[:, 1:2], None, ALU.mult, None, ["sti1"], ["Bst"])
                        for n in range(NT - 1, -1, -1):
                            pk = pkv[cnt % 2]
                            pkt = f"pkv{cnt % 2}"
                            cnt += 1
                            ACOPY(SB[:, n, :], Bst[:], ["Bst"], ["SB"])
                            if n > 0:
                                MM(pk[:], kt[k][:, n, :], vzb[:, n, :], True, True, ktg + ["vzb"], [pkt])
                                STT("dve", Bst[:], Bst[:], zc[:, 3:4], pk[:], ALU.mult, ALU.add, [pkt, "zc3", "Bst"], ["Bst"])
                        for n in range(NT):
                            ns = slice(n * 128, (n + 1) * 128)
                            b2 = n % 2
                            CP("pool", Fbf[b2][:], Fst[:], ["Fst"], [f"Fbf{b2}"])
                            MM(psc[b2][:], kTr[k][:, ns], qTr[k][:, ns], True, True, [f"kTr{k}", f"qTr{k}"], [f"psc{b2}"])
                            TT("dve", scD[b2][:], psc[b2][:], Dm[:], ALU.mult, [f"psc{b2}", "Dm"], [f"scD{b2}"])
                            MM(pov[b2][:], scD[b2][:], vt[k][:, n, :], True, False, [f"scD{b2}"] + vtg, [f"pov{b2}"])
                            MM(pov[b2][:], qxf[:, n, :], Fbf[b2][:], False, False, ["qxf", f"Fbf{b2}"], [f"pov{b2}"])
                            MM(pov[b2][:], qxb[:, n, :], SB[:, n, :], False, True, ["qxb", "SB"], [f"pov{b2}"])
                            if n < NT - 1:
                                pk = pkv[cnt % 2]
                                pkt = f"pkv{cnt % 2}"
                                cnt += 1
                                MM(pk[:], kt[k][:, n, :], vzf[:, n, :], True, True, ktg + ["vzf"], [pkt])
                                STT("dve", Fst[:], Fst[:], zc[:, 2:3], pk[:], ALU.mult, ALU.add, [pkt, "zc2", "Fst"], ["Fst"])
                            RSUM("dve", gs[:, 0:1], pov[b2][:], [f"pov{b2}"], ["gs0"])
                            TS("dve", gs[:, 1:2], gs[:, 0:1], -1.0 / 128, None, ALU.mult, None, ["gs0"], ["gs1"])
                            TS("dve", xc[:], pov[b2][:], gs[:, 1:2], None, ALU.add, None, [f"pov{b2}", "gs1"], ["xc"])
                            ACT(jk[:], xc[:], AF.Square, ["xc"], ["jk", "gs2"], accum=gs[:, 2:3])
                            TS("dve", gs[:, 3:4], gs[:, 2:3], 1.0 / 128, 1e-5, ALU.mult, ALU.add, ["gs2"], ["gs3"])
                            TT("pool", gs[:, 3:4], gs[:, 3:4], mhalf[:, 0:1], ALU.pow, ["gs3"], ["gs4"])
                            STT("dve", yt[:], xc[:], gs[:, 3:4], rgn[:, h * 128:(h + 1) * 128], ALU.mult, ALU.mult,
                                ["xc", "gs4"], ["yt"])
                            TT("pool", rb[b2][:], yt[:], gt[k][:, n, :], ALU.mult, ["yt"] + gtg, [f"rb{b2}"])
                            TR(ptt[b2][:], rb[b2][:], ident[:], [f"rb{b2}"], [f"ptt{b2}"])
                            rk_ = (n // 4) % 2
                            ACOPY(roT[rk_][:, (n % 4) * 128:(n % 4 + 1) * 128], ptt[b2][:], [f"ptt{b2}"], [f"roT{rk_}"])
                            if n % 4 == 3:
                                tb = n // 4
                                DMA(BR[1, h * 128:(h + 1) * 128, tb * 512:(tb + 1) * 512], roT[rk_][:], [f"roT{rk_}"], [])
                    S.barrier()
                if dbg and l == 0 and stop >= 5:
                    DMA(dbg_out["d_BR"], BR, (), [])
                    S.barrier()

                for ph in phase_ctx(6):
                    Wb = sbT(ph, "Wb", [128, 12, D], BF16)
                    Wo = sbT(ph, "Wo", [128, 8, D], BF16)
                    wst = [sbT(ph, "wstd", [128, 4, D]) for _ in range(2)]
                    for i in range(3):
                        k = i % 2
                        DMA(wst[k][:], w_branch[l, i, :, :].rearrange("(c p) n -> p c n", p=128), (), [f"wst{k}"])
                        CP("pool", Wb[:, 4 * i:4 * i + 4, :], wst[k][:], [f"wst{k}"], ["Wb"])
                    for i in range(2):
                        k = (3 + i) % 2
                        DMA(wst[k][:], w_out[l, i * 512:(i + 1) * 512, :].rearrange("(c p) n -> p c n", p=128), (), [f"wst{k}"])
                        CP("pool", Wo[:, 4 * i:4 * i + 4, :], wst[k][:], [f"wst{k}"], ["Wo"])
                    brT = [sbT(ph, "brT", [128, 12, 512], BF16) for _ in range(2)]
                    gtile = [sbT(ph, "gtile", [128, 3072], BF16) for _ in range(2)]
                    ht = [sbT(ph, "htd", [128, D]) for _ in range(2)]
                    mg = sbT(ph, "mg", [128, D])
                    mt = sbT(ph, "mt", [128, D])
                    mgb = sbT(ph, "mgb", [128, D], BF16)
                    mT = sbT(ph, "mT", [128, 8, 128], BF16)
                    h1 = [sbT(ph, "h1", [128, D]) for _ in range(2)]
                    junk = sbT(ph, "junkd", [128, D], BF16)
                    ss1 = sbT(ph, "ss1d", [128, 2])
                    u2 = sbT(ph, "u2", [128, D], BF16)
                    u2T = [sbT(ph, "u2T", [128, 8, 512], BF16) for _ in range(2)]
                    pj = [psT(ph, "pj", [128, 512]) for _ in range(4)]
                    ptr = [psT(ph, "ptrd", [128, 8, 128], BF16) for _ in range(2)]
                    for tb in range(8):
                        bk = tb % 2
                        DMA(brT[bk][:], BR[:, :, tb * 512:(tb + 1) * 512].rearrange("n (c p) t -> p (n c) t", p=128),
                            (), [f"brT{bk}"])
                        for t4 in range(4):
                            ti = tb * 4 + t4
                            k = ti % 2
                            rows = slice(ti * 128, (ti + 1) * 128)
                            ts_ = slice(t4 * 128, (t4 + 1) * 128)
                            DMA(gtile[k][:], GATES[rows, :], (), [f"gtile{k}"])
                            DMA(ht[k][:], Hsrc[rows, :], (), [f"htd{k}"])
                            for n in range(3):
                                for hf in range(2):
                                    pjk = pj[hf]
                                    for c in range(4):
                                        MM(pjk[:], brT[bk][:, 4 * n + c, ts_], Wb[:, 4 * n + c, hf * 512:(hf + 1) * 512],
                                           c == 0, c == 3, [f"brT{bk}", "Wb"], [f"pj{hf}"])
                                    hs = slice(hf * 512, (hf + 1) * 512)
                                    gsl = slice(n * D + hf * 512, n * D + (hf + 1) * 512)
                                    if n == 0:
                                        TT("dve", mg[:, hs], pjk[:], gtile[k][:, gsl], ALU.mult, [f"pj{hf}", f"gtile{k}"], [f"mg{hf}"])
                                    else:
                                        TT("dve", mt[:, hs], pjk[:], gtile[k][:, gsl], ALU.mult, [f"pj{hf}", f"gtile{k}"], [f"mt{hf}"])
                                        TT("pool", mg[:, hs], mg[:, hs], mt[:, hs], ALU.add, [f"mg{hf}", f"mt{hf}"], [f"mg{hf}"])
                            ACOPY(mgb[:], mg[:], ["mg0", "mg1"], ["mgb"])
                            for c in range(8):
                                TR(ptr[0][:, c, :], mgb[:, c * 128:(c + 1) * 128], ident[:], ["mgb"], ["ptr0"])
                            ACOPY(mT[:], ptr[0][:], ["ptr0"], ["mT"])
                            for hf in range(2):
                                pjk = pj[2 + hf]
                                for c in range(8):
                                    MM(pjk[:], mT[:, c, :], Wo[:, c, hf * 512:(hf + 1) * 512], c == 0, c == 7, ["mT", "Wo"], [f"pj{2 + hf}"])
                                hs = slice(hf * 512, (hf + 1) * 512)
                                TT("dve", h1[k][:, hs], pjk[:], ht[k][:, hs], ALU.add, [f"pj{2 + hf}", f"htd{k}"], [f"h1{k}{hf}"])
                            DMA(H1[rows, :], h1[k][:], [f"h1{k}0", f"h1{k}1"], [f"h1st{k}"])
                            ACT(junk[:], h1[k][:], AF.Square, [f"h1{k}0", f"h1{k}1"], ["junk", "ssa"], accum=ss1[:, 0:1])
                            TS("dve", ss1[:, 1:2], ss1[:, 0:1], 1.0 / D, EPS, ALU.mult, ALU.add, ["ssa"], ["ssb"])
                            TT("pool", ss1[:, 1:2], ss1[:, 1:2], mhalf[:, 0:1], ALU.pow, ["ssb"], ["ssc"])
                            STT("dve", u2[:], h1[k][:], ss1[:, 1:2], g_mlp[:], ALU.mult, ALU.mult,
                                [f"h1{k}0", f"h1{k}1", "ssc"], ["u2"])
                            for c in range(8):
                                TR(ptr[1][:, c, :], u2[:, c * 128:(c + 1) * 128], ident[:], ["u2"], ["ptr1"])
                            ACOPY(u2T[bk][:, :, ts_], ptr[1][:], ["ptr1"], [f"u2T{bk}"])
                        DMA(U2T[:, tb * 512:(tb + 1) * 512].rearrange("(c p) t -> p c t", p=128), u2T[bk][:], [f"u2T{bk}"], [f"u2Tst{bk}"])
                    S.barrier()
                if dbg and l == 0 and stop >= 6:
                    DMA(dbg_out["d_H1"], H1, (), [])
                    S.barrier()

                for ph in phase_ctx(7):
                    W1 = sbT(ph, "W1", [128, 8, DFF], BF16)
                    W2 = sbT(ph, "W2", [128, 32, D], BF16)
                    wst = [sbT(ph, "wste", [128, 2, D]) for _ in range(2)]
                    wi = 0
                    for c in range(8):
                        for hh in range(2):
                            k = wi % 2
                            wi += 1
                            DMA(wst[k][:, :, :].rearrange("p a b -> p (a b)"),
                                w_mlp_in[l, c * 128:(c + 1) * 128, hh * 2048:(hh + 1) * 2048], (), [f"wst{k}"])
                            CP("pool", W1[:, c, hh * 2048:(hh + 1) * 2048], wst[k][:, :, :].rearrange("p a b -> p (a b)"),
                               [f"wst{k}"], ["W1"])
                    for i in range(16):
                        k = wi % 2
                        wi += 1
                        DMA(wst[k][:], w_mlp_out[l, i * 256:(i + 1) * 256, :].rearrange("(c p) n -> p c n", p=128), (), [f"wst{k}"])
                        CP("pool", W2[:, 2 * i:2 * i + 2, :], wst[k][:], [f"wst{k}"], ["W2"])
                    TB = 256
                    u2T = [sbT(ph, "u2Te", [128, 8, TB], BF16) for _ in range(2)]
                    mTt = sbT(ph, "mTt", [128, 32, TB], BF16)
                    rl = [sbT(ph, "rl", [128, TB], BF16) for _ in range(2)]
                    h1 = [sbT(ph, "h1e", [128, D]) for _ in range(2)]
                    pm = [psT(ph, "pm", [128, 512]) for _ in range(2)]
                    pq = [psT(ph, "pq", [128, 512]) for _ in range(4)]

                    def loadU(tb):
                        bk = tb % 2
                        DMA(u2T[bk][:], U2T[:, tb * TB:(tb + 1) * TB].rearrange("(c p) t -> p c t", p=128), (), [f"u2T{bk}"])

                    loadU(0)
                    for tb in range(T // TB):
                        bk = tb % 2
                        if tb + 1 < T // TB:
                            loadU(tb + 1)
                        for fc in range(32):
                            k = fc % 2
                            for c in range(8):
                                MM(pm[k][:, 0:TB], W1[:, c, fc * 128:(fc + 1) * 128], u2T[bk][:, c, :], c == 0, c == 7,
                                   ["W1", f"u2T{bk}"], [f"pm{k}"])
                            ACT(rl[k][:], pm[k][:, 0:TB], AF.Relu, [f"pm{k}"], [f"rl{k}"])
                            TT("dve", mTt[:, fc, :], rl[k][:], rl[k][:], ALU.mult, [f"rl{k}"], ["mTt"])
                        for t4 in range(TB // 128):
                            ti = tb * (TB // 128) + t4
                            k = ti % 2
                            rows = slice(ti * 128, (ti + 1) * 128)
                            DMA(h1[k][:], H1[rows, :], (), [f"h1e{k}", f"ho{k}0", f"ho{k}1"])
                            for hf in range(2):
                                pqk = pq[2 * k + hf]
                                for fc in range(32):
                                    MM(pqk[:], mTt[:, fc, t4 * 128:(t4 + 1) * 128], W2[:, fc, hf * 512:(hf + 1) * 512],
                                       fc == 0, fc == 31, ["mTt", "W2"], [f"pq{2 * k + hf}"])
                                hs = slice(hf * 512, (hf + 1) * 512)
                                TT("dve", h1[k][:, hs], pqk[:], h1[k][:, hs], ALU.add, [f"pq{2 * k + hf}", f"h1e{k}"], [f"ho{k}{hf}"])
                            DMA(Hdst[rows, :], h1[k][:], [f"ho{k}0", f"ho{k}1"], [f"host{k}"])
                    S.barrier()
        S.emit(top)
    return nc


def _rope_tab(pos, dim, theta):
    inv = theta ** (-np.arange(0, dim, 2, dtype=np.float32) / np.float32(dim))
    ang = pos.astype(np.float32)[:, None] * inv[None, :].astype(np.float32)
    return np.cos(ang).astype(np.float32), np.sin(ang).astype(np.float32)


def _consts(half):
    pos = np.arange(half * T, (half + 1) * T, dtype=np.float32)
    c, s = _rope_tab(pos, 16, 500000.0)
    c_rp = np.concatenate([c, s], axis=1).astype(np.float32)
    c, s = _rope_tab(pos, 128, 10000.0)
    sc = np.float32(128 ** -0.5)
    c_rr = np.concatenate([c, s, c * sc, s * sc], axis=1).astype(np.float32)
    rc, rs_ = _rope_tab(np.floor(pos / 64), 32, 10000.0)
    cc, cs = _rope_tab(np.mod(pos, 64), 32, 10000.0)
    c_ax = np.concatenate([rc, cc, rs_, cs], axis=1).astype(np.float32)
    j = np.arange(128, dtype=np.float32)[:, None]
    i = np.arange(128, dtype=np.float32)[None, :]
    P1 = np.maximum(i - j, 0.0) + 0 * j
    P2 = np.maximum(j - i, 0.0) + 0 * i
    IF1 = (i + 1.0) + 0 * j
    IF2 = (128.0 - i) + 0 * j
    ZF = 127.0 - j
    ZB = j + 0.0
    n = np.arange(32, dtype=np.float32)[None, :]
    T1 = 4095.0 - 128.0 * n - j
    T2 = 128.0 * n + j
    sel = np.zeros((128, 64), np.float32)
    sel[64, :] = 1.0
    c_pat = np.concatenate([P1, P2, IF1, IF2, ZF, ZB, T1, T2, sel], axis=1).astype(np.float32)
    flags = np.zeros((128, 2), np.float32)
    flags[:, 0] = 1.0 if half == 1 else 0.0
    flags[:, 1] = 1.0 if half == 0 else 0.0
    return {"c_ident": np.eye(128, dtype=np.float32), "c_rp": c_rp, "c_rr": c_rr, "c_ax": c_ax,
            "c_pat": c_pat, "c_flags": flags}


_PARAMS = ["attn_norm", "w_in", "diff_q_norm", "diff_k_norm", "diff_lam_q1", "diff_lam_k1", "diff_lam_q2",
           "diff_lam_k2", "diff_subln", "ret_decay_fwd", "ret_decay_bwd", "ret_group_norm", "gqa_q_norm",
           "gqa_k_norm", "w_branch", "w_out", "mlp_norm", "w_mlp_in", "w_mlp_out"]


def make_in_maps(inputs):
    x = np.asarray(inputs["x"], dtype=np.float32)
    params = {k: np.ascontiguousarray(np.asarray(inputs[k], dtype=np.float32)) for k in _PARAMS}
    cst = [_consts(0), _consts(1)]
    in_maps = []
    for c in range(NCORES):
        b, half = c // 2, c % 2
        m = {"x": np.ascontiguousarray(x[b, half * T:(half + 1) * T, :])}
        m.update(params)
        m.update(cst[half])
        in_maps.append(m)
    return in_maps


def kernel(**inputs):
    nc = build(2)
    in_maps = make_in_maps(inputs)
    res = run_bass_kernel_spmd(nc, in_maps, core_ids=list(range(NCORES)))
    out = np.empty((4, SEQ, D), np.float32)
    for c in range(NCORES):
        b, half = c // 2, c % 2
        out[b, half * T:(half + 1) * T, :] = res.results[c]["out"]
    return out
```
